# Optimizing a Trainium2 kernel written in Bass

```python
import math
import jax, jax.numpy as jnp
from jax import lax
import numpy as np

D_MODEL = 1024
BATCH = 8
SEQ = 2048
DEPTH = 1
DEC_BATCH = 128
DEC_SEQ = 1
PAST_LEN = 16384
PAGE_SIZE = 128

D_CONV_A = D_MODEL
K_A = 3
EXPAND = 2
D_INNER = EXPAND * D_MODEL
HEAD_DIM = 64
N_HEADS = D_INNER // HEAD_DIM
N_GROUPS = 4
HEADS_PER_GROUP = N_HEADS // N_GROUPS
D_STATE = 128
K_B = 4
CHUNK = 128
D_XBC = D_INNER + 2 * N_GROUPS * D_STATE
D_FF = 4 * D_MODEL
EPS = 1e-6

_SECTION_WIDTHS = [D_CONV_A, D_CONV_A, D_CONV_A, D_INNER, D_XBC, N_HEADS, D_MODEL, D_MODEL]
SPLIT_POINTS = [int(v) for v in np.cumsum(_SECTION_WIDTHS)[:-1]]
D_IN_PROJ = int(sum(_SECTION_WIDTHS))

kernel_name = 'hybrid_shortconv_ssd_decoder_step'


def rmsnorm(x, g):
    xf = x.astype(jnp.float32)
    ms = jnp.mean(xf * xf, axis=-1, keepdims=True)
    return (xf * lax.rsqrt(ms + EPS) * g.astype(jnp.float32)).astype(x.dtype)


def group_rmsnorm(y, g):
    b, L, d = y.shape
    yf = y.astype(jnp.float32).reshape(b, L, N_GROUPS, d // N_GROUPS)
    ms = jnp.mean(yf * yf, axis=-1, keepdims=True)
    yn = (yf * lax.rsqrt(ms + EPS)).reshape(b, L, d)
    return (yn * g.astype(jnp.float32)).astype(y.dtype)


def causal_conv(u, prev, w, bias=None):
    K = w.shape[0]
    L = u.shape[1]
    full = jnp.concatenate([prev.astype(u.dtype), u], axis=1)
    out = full[:, 0:L] * w[0]
    for k in range(1, K):
        out = out + full[:, k:k + L] * w[k]
    if bias is not None:
        out = out + bias
    return out, full[:, L:]


def ssd(xh, dt, a, bm, cm, h0):
    b, L = xh.shape[0], xh.shape[1]
    q = min(CHUNK, L)
    nc = -(-L // q)
    pad = nc * q - L
    f32 = jnp.float32
    if pad:
        padt = lambda t: jnp.pad(t, [(0, 0), (0, pad)] + [(0, 0)] * (t.ndim - 2))
        xh, dt, bm, cm = padt(xh), padt(dt), padt(bm), padt(cm)
    G, R, P, N = N_GROUPS, HEADS_PER_GROUP, HEAD_DIM, D_STATE
    x = xh.reshape(b, nc, q, G, R, P).astype(f32)
    dtc = dt.reshape(b, nc, q, G, R)
    B = bm.reshape(b, nc, q, G, N).astype(f32)
    C = cm.reshape(b, nc, q, G, N).astype(f32)
    acs = jnp.cumsum(dtc * a.reshape(G, R), axis=2)
    acs_t = jnp.moveaxis(acs, 2, -1)
    seg = acs_t[..., :, None] - acs_t[..., None, :]
    causal = jnp.tril(jnp.ones((q, q), dtype=bool))
    lmat = jnp.exp(jnp.where(causal, seg, -jnp.inf))
    cb = jnp.einsum('bcqgn,bcsgn->bcgqs', C, B)
    wts = cb[:, :, :, None] * lmat * jnp.moveaxis(dtc, 2, -1)[..., None, :]
    y_diag = jnp.einsum('bcgrqs,bcsgrp->bcqgrp', wts, x)
    decay_end = jnp.exp(acs[:, :, -1:] - acs) * dtc
    states = jnp.einsum('bcsgn,bcsgr,bcsgrp->bcgrpn', B, decay_end, x)
    chunk_decay = jnp.exp(acs[:, :, -1])

    def step(h, inp):
        dec, st = inp
        return dec[..., None, None] * h + st, h

    h_last, h_in = lax.scan(step, h0.reshape(b, G, R, P, N).astype(f32),
                            (jnp.moveaxis(chunk_decay, 1, 0), jnp.moveaxis(states, 1, 0)))
    y_off = jnp.einsum('bcqgn,bcqgr,cbgrpn->bcqgrp', C, jnp.exp(acs), h_in)
    y = (y_diag + y_off).reshape(b, nc * q, N_HEADS, P)[:, :L]
    return y, h_last.reshape(b, N_HEADS, P, N)


def layer(x, c, st_a, st_bconv, st_ssm, w_ada, b_ada, norm1_g, w_in, conv_a_w, w_a_out,
          conv_b_w, conv_b_b, dt_bias, a_log, d_skip, ssm_norm_g, w_b_out, w_o,
          norm2_g, w_mlp1, w_mlp2):
    b_, L = x.shape[0], x.shape[1]
    mod = jax.nn.silu(c) @ w_ada + b_ada
    sh1, sc1, g1, sh2, sc2, g2 = jnp.split(mod[:, None, :], 6, axis=-1)
    u = rmsnorm(x, norm1_g) * (1 + sc1) + sh1
    proj = u @ w_in
    bgate, cgate, hval, z, xbc, dt_raw, ga, gb = jnp.split(proj, SPLIT_POINTS, axis=-1)
    conv_out, new_a = causal_conv(cgate * hval, st_a, conv_a_w)
    y_a = (bgate * conv_out) @ w_a_out
    xbc_c, new_bconv = causal_conv(xbc, st_bconv, conv_b_w, conv_b_b)
    xbc_c = jax.nn.silu(xbc_c)
    xs, bm, cm = jnp.split(xbc_c, [D_INNER, D_INNER + N_GROUPS * D_STATE], axis=-1)
    dt = jax.nn.softplus(dt_raw.astype(jnp.float32) + dt_bias.astype(jnp.float32))
    a = -jnp.exp(a_log.astype(jnp.float32))
    xh = xs.reshape(b_, L, N_HEADS, HEAD_DIM)
    y_s, new_ssm = ssd(xh, dt, a, bm.reshape(b_, L, N_GROUPS, D_STATE),
                       cm.reshape(b_, L, N_GROUPS, D_STATE), st_ssm)
    y_s = y_s + d_skip.astype(jnp.float32)[:, None] * xh.astype(jnp.float32)
    y_s = y_s.reshape(b_, L, D_INNER).astype(x.dtype) * jax.nn.silu(z)
    y_b = group_rmsnorm(y_s, ssm_norm_g) @ w_b_out
    merged = jax.nn.sigmoid(ga) * y_a + jax.nn.sigmoid(gb) * y_b
    x = x + g1 * (merged @ w_o)
    u2 = rmsnorm(x, norm2_g) * (1 + sc2) + sh2
    hmid = jnp.square(jax.nn.relu(u2 @ w_mlp1))
    x = x + g2 * (hmid @ w_mlp2)
    return x, new_a.astype(st_a.dtype), new_bconv.astype(st_bconv.dtype), new_ssm.astype(st_ssm.dtype)


def setup_inputs(seed: int = 0) -> dict:
    key = jax.random.key(seed)
    ks = jax.random.split(key, 32)
    f32 = jnp.float32

    def nrm(k, shape, scale):
        return jax.random.normal(k, shape, f32) * scale

    Ld = DEPTH
    dt0 = jnp.exp(jax.random.uniform(ks[10], (Ld, N_HEADS), f32, math.log(1e-3), math.log(1e-1)))
    return {
        'x_prompt': nrm(ks[0], (BATCH, SEQ, D_MODEL), 1.0),
        'x_sample': nrm(ks[1], (DEC_BATCH, DEC_SEQ, D_MODEL), 1.0),
        'c_prompt': nrm(ks[2], (BATCH, D_MODEL), 1.0),
        'c_sample': nrm(ks[3], (DEC_BATCH, D_MODEL), 1.0),
        'state_shortconv': nrm(ks[4], (Ld, DEC_BATCH, K_A - 1, D_CONV_A), 1.0),
        'state_ssm_conv': nrm(ks[5], (Ld, DEC_BATCH, K_B - 1, D_XBC), 1.0),
        'state_ssm': nrm(ks[6], (Ld, DEC_BATCH, N_HEADS, HEAD_DIM, D_STATE), 0.5),
        'w_ada': nrm(ks[7], (Ld, D_MODEL, 6 * D_MODEL), 0.5 * D_MODEL ** -0.5),
        'b_ada': nrm(ks[8], (Ld, 6 * D_MODEL), 0.02),
        'norm1_g': 1.0 + nrm(ks[9], (Ld, D_MODEL), 0.05),
        'w_in': nrm(ks[11], (Ld, D_MODEL, D_IN_PROJ), D_MODEL ** -0.5),
        'conv_a_w': nrm(ks[12], (Ld, K_A, D_CONV_A), K_A ** -0.5),
        'w_a_out': nrm(ks[13], (Ld, D_CONV_A, D_MODEL), D_CONV_A ** -0.5),
        'conv_b_w': nrm(ks[14], (Ld, K_B, D_XBC), K_B ** -0.5),
        'conv_b_b': nrm(ks[15], (Ld, D_XBC), 0.02),
        'dt_bias': dt0 + jnp.log(-jnp.expm1(-dt0)),
        'a_log': jnp.log(jax.random.uniform(ks[16], (Ld, N_HEADS), f32, 1.0, 16.0)),
        'd_skip': 1.0 + nrm(ks[17], (Ld, N_HEADS), 0.1),
        'ssm_norm_g': 1.0 + nrm(ks[18], (Ld, D_INNER), 0.05),
        'w_b_out': nrm(ks[19], (Ld, D_INNER, D_MODEL), D_INNER ** -0.5),
        'w_o': nrm(ks[20], (Ld, D_MODEL, D_MODEL), D_MODEL ** -0.5),
        'norm2_g': 1.0 + nrm(ks[21], (Ld, D_MODEL), 0.05),
        'w_mlp1': nrm(ks[22], (Ld, D_MODEL, D_FF), D_MODEL ** -0.5),
        'w_mlp2': nrm(ks[23], (Ld, D_FF, D_MODEL), D_FF ** -0.5),
        'norm_f_g': 1.0 + nrm(ks[24], (D_MODEL,), 0.05),
    }


def reference(x_prompt, x_sample, c_prompt, c_sample, state_shortconv, state_ssm_conv, state_ssm,
              w_ada, b_ada, norm1_g, w_in, conv_a_w, w_a_out, conv_b_w, conv_b_b, dt_bias, a_log,
              d_skip, ssm_norm_g, w_b_out, w_o, norm2_g, w_mlp1, w_mlp2, norm_f_g):
    yp, ys = x_prompt, x_sample
    bp = x_prompt.shape[0]
    pa, pbc, ps, sa, sbc, ss = [], [], [], [], [], []
    for l in range(DEPTH):
        lw = (w_ada[l], b_ada[l], norm1_g[l], w_in[l], conv_a_w[l], w_a_out[l], conv_b_w[l],
              conv_b_b[l], dt_bias[l], a_log[l], d_skip[l], ssm_norm_g[l], w_b_out[l], w_o[l],
              norm2_g[l], w_mlp1[l], w_mlp2[l])
        z_a = jnp.zeros((bp, K_A - 1, D_CONV_A), state_shortconv.dtype)
        z_bc = jnp.zeros((bp, K_B - 1, D_XBC), state_ssm_conv.dtype)
        z_s = jnp.zeros((bp, N_HEADS, HEAD_DIM, D_STATE), state_ssm.dtype)
        yp, na, nbc, ns = layer(yp, c_prompt, z_a, z_bc, z_s, *lw)
        pa.append(na); pbc.append(nbc); ps.append(ns)
        ys, na, nbc, ns = layer(ys, c_sample, state_shortconv[l], state_ssm_conv[l], state_ssm[l], *lw)
        sa.append(na); sbc.append(nbc); ss.append(ns)
    y_prompt = rmsnorm(yp, norm_f_g)
    y_sample = rmsnorm(ys, norm_f_g)
    return (y_prompt, y_sample, jnp.stack(pa), jnp.stack(pbc), jnp.stack(ps),
            jnp.stack(sa), jnp.stack(sbc), jnp.stack(ss))
```

```python
import numpy as np
from contextlib import ExitStack
import concourse.bass as bass
import concourse.mybir as mybir
from concourse.bass_utils import run_bass_kernel_spmd

F32 = mybir.dt.float32
BF16 = mybir.dt.bfloat16
AF = mybir.ActivationFunctionType
ALU = mybir.AluOpType
AX = mybir.AxisListType

NCORES = 8
D = 1024
T = 512
import os
NTILE = int(os.environ.get('K_NTILE', '4'))
SEMMAX = int(os.environ.get('K_SEMMAX', '30000'))
PHSTOP = int(os.environ.get('K_PHSTOP', '99'))
DIN = 10272
O_BG, O_CG, O_HV, O_Z, O_XBC, O_DT, O_GA, O_GB = 0, 1024, 2048, 3072, 5120, 8192, 8224, 9248
EPS = 1e-6
P_N1G, P_CAW, P_CBW, P_CBB, P_SNG, P_N2G, P_BADA, P_NFG, P_HP, NPF = 0, 8, 32, 128, 152, 168, 176, 224, 232, 235
R_DTB, R_ALOG, R_DSK, R_NFG, NPR = 0, 32, 64, 96, 1120
NSB = 16


class Tr:
    def __init__(self, nc, es):
        self.nc, self.es = nc, es
        self.eng = {'pe': nc.tensor, 'act': nc.scalar, 'dve': nc.vector, 'pool': nc.gpsimd, 'sp': nc.sync}
        self.cnt = {e: 0 for e in self.eng}
        self.semi = {e: 0 for e in self.eng}
        self.sem = {}
        self.semname = {}
        for e in self.eng:
            self._newsem(e)
        self.seen = {e: {} for e in self.eng}
        self.lastw = {}
        self.reads = {}
        self.base = {}
        self.dsem = {}
        self.nwait = 0
        self.alias = {}

    def _canon(self, keys):
        return [self.alias.get(k, k) for k in keys]

    @staticmethod
    def _psum_excl(reads, writes):
        pr_ = [k for k in reads if k[0] in ('psf', 'psb')]
        if pr_:
            reads = [k for k in reads if k[0] not in ('psf', 'psb')]
            writes = list(writes) + pr_
        return reads, writes

    def _newsem(self, e):
        n = f"c_{e}_{self.semi[e]}"
        self.semi[e] += 1
        self.sem[e] = self.es.enter_context(self.nc.semaphore(n))
        self.semname[e] = n
        self.cnt[e] = 0

    def _need(self, e, dep, force, psum=False):
        name, sem, val, src = dep
        if src == e and not force and (e == 'pe' or psum):
            return
        if self.seen[e].get(name, 0) >= val:
            return
        self.eng[e].wait_ge(sem, val)
        self.nwait += 1
        self.seen[e][name] = val

    def _deps(self, e, reads, writes, isdma):
        for k in list(reads) + list(writes):
            b = self.base.get(k[0])
            if b:
                for d in b.values():
                    self._need(e, d, True)
        for k in reads:
            if k in self.lastw:
                self._need(e, self.lastw[k], True)
        for k in writes:
            ps_ = k[0] in ('psf', 'psb')
            if k in self.lastw:
                self._need(e, self.lastw[k], isdma, ps_)
            for d in self.reads.get(k, {}).values():
                self._need(e, d, isdma, ps_)

    def _commit(self, dep, reads, writes):
        for k in reads:
            self.reads.setdefault(k, {})[dep[0]] = dep
        for k in writes:
            self.lastw[k] = dep
            self.reads[k] = {}

    def op(self, e, fn, reads=(), writes=(), inc=True):
        reads, writes = self._canon(reads), self._canon(writes)
        reads, writes = self._psum_excl(reads, writes)
        if inc and self.cnt[e] >= SEMMAX:
            self._newsem(e)
        self._deps(e, reads, writes, False)
        ins = fn(self.eng[e])
        if inc:
            self.cnt[e] += 1
            ins.then_inc(self.sem[e], 1)
            val = self.cnt[e]
        else:
            val = self.cnt[e] + 1
        self._commit((self.semname[e], self.sem[e], val, e), reads, writes)

    def dma(self, q, out, in_, semname, reads=(), writes=(), **kw):
        reads, writes = self._canon(reads), self._canon(writes)
        self._deps(q, reads, writes, True)
        if semname not in self.dsem:
            self.dsem[semname] = [self.es.enter_context(self.nc.semaphore(semname)), 0]
        s = self.dsem[semname]
        s[1] += 16
        self.eng[q].dma_start(out=out, in_=in_, **kw).then_inc(s[0], 16)
        self._commit((semname, s[0], s[1], 'dma'), reads, writes)

    def dma_multi(self, q, items, semname):
        for (out, in_, writes) in items:
            self._deps(q, (), writes, True)
        if semname not in self.dsem:
            self.dsem[semname] = [self.es.enter_context(self.nc.semaphore(semname)), 0]
        s = self.dsem[semname]
        for (out, in_, writes) in items:
            s[1] += 16
            self.eng[q].dma_start(out=out, in_=in_).then_inc(s[0], 16)
        for (out, in_, writes) in items:
            self._commit((semname, s[0], s[1], 'dma'), (), writes)

    def retire(self, buf):
        b = self.base.setdefault(buf, {})
        for k in list(self.lastw):
            if k[0] == buf:
                d = self.lastw.pop(k)
                if d[0] not in b or b[d[0]][2] < d[2]:
                    b[d[0]] = d
        for k in list(self.reads):
            if k[0] == buf:
                for d in self.reads.pop(k).values():
                    if d[0] not in b or b[d[0]][2] < d[2]:
                        b[d[0]] = d

    def barrier(self):
        for e in self.eng:
            for f in self.eng:
                if f != e and self.cnt[f] > 0:
                    self._need(e, (self.semname[f], self.sem[f], self.cnt[f], f), True)
            for name, (sem, val) in self.dsem.items():
                self._need(e, (name, sem, val, 'dma'), True)
        self.lastw.clear()
        self.reads.clear()
        self.base.clear()

    def finish(self, e='sp'):
        for name, (sem, val) in self.dsem.items():
            if self.seen[e].get(name, 0) < val:
                self.eng[e].wait_ge(sem, val)


def build():
    nc = bass.Bass("TRN2", target_bir_lowering=False)
    es = ExitStack()
    tr = Tr(nc, es)

    def din(n, s):
        return nc.dram_tensor(n, list(s), F32, kind="ExternalInput").ap()

    def dout(n, s):
        return nc.dram_tensor(n, list(s), F32, kind="ExternalOutput").ap()

    def sb(n, s, dt=F32):
        return es.enter_context(nc.sbuf_tensor(n, list(s), dt))

    A = tr.op
    xp = din("xp", [2048, D])
    cpT = din("cpT", [128, 8])
    pfm = din("pfm", [128, NPF])
    prow = din("prow", [1, NPR])
    bg12 = din("bg12", [2, D])
    w_ada = din("w_ada", [D, 6 * D])
    w_in = din("w_in", [D, DIN])
    w_a = din("w_a_out", [D, D])
    w_b = din("w_b_out", [2 * D, D])
    w_o = din("w_o", [D, D])
    w1 = din("w_mlp1", [D, 4 * D])
    w2 = din("w_mlp2", [4 * D, D])
    yp = dout("yp", [2048, D])
    na_p = dout("na_p", [2, D])
    nbc_p = dout("nbc_p", [3, 3072])
    nssm_p = dout("nssm_p", [2048, 128])

    ident_b = sb("ident_b", [128, 128], BF16)
    ident_f = sb("ident_f", [128, 128])
    U = sb("U", [128, 128])
    ones_f = sb("ones_f", [128, 128])
    Esel = sb("Esel", [32, 32 * 128], BF16)
    A('pool', lambda e: e.memset(ident_b[:], 0.0), writes=[('ident_b',)])
    A('pool', lambda e: e.affine_select(out=ident_b[:], in_=ident_b[:], pattern=[[-1, 128]], compare_op=ALU.not_equal,
                                        fill=1.0, base=0, channel_multiplier=1), reads=[('ident_b',)], writes=[('ident_b',)])
    A('pool', lambda e: e.memset(ident_f[:], 0.0), writes=[('ident_f',)])
    A('pool', lambda e: e.affine_select(out=ident_f[:], in_=ident_f[:], pattern=[[-1, 128]], compare_op=ALU.not_equal,
                                        fill=1.0, base=0, channel_multiplier=1), reads=[('ident_f',)], writes=[('ident_f',)])
    A('pool', lambda e: e.memset(U[:], 1.0), writes=[('U',)])
    A('pool', lambda e: e.affine_select(out=U[:], in_=U[:], pattern=[[1, 128]], compare_op=ALU.is_ge,
                                        fill=0.0, base=0, channel_multiplier=-1), reads=[('U',)], writes=[('U',)])
    A('pool', lambda e: e.memset(ones_f[:], 1.0), writes=[('ones_f',)])
    NEGM = sb("NEGM", [128, 512], BF16)
    A('pool', lambda e: e.memset(NEGM[:], -30000.0), writes=[('NEGM',)])
    nv = NEGM[:].rearrange("p (a b) -> p a b", b=128)
    A('pool', lambda e: e.affine_select(out=nv, in_=nv, pattern=[[0, 4], [-1, 128]], compare_op=ALU.is_gt,
                                        fill=0.0, base=0, channel_multiplier=1), reads=[('NEGM',)], writes=[('NEGM',)])
    A('pool', lambda e: e.memset(Esel[:], 0.0), writes=[('Esel',)])
    ev = Esel[:].rearrange("p (h s) -> p h s", s=128)
    A('pool', lambda e: e.affine_select(out=ev, in_=ev, pattern=[[1, 32], [0, 128]], compare_op=ALU.not_equal,
                                        fill=1.0, base=0, channel_multiplier=-1), reads=[('Esel',)], writes=[('Esel',)])

    NPSF = 6
    psf = [es.enter_context(nc.psum_tensor(f"psf{i}", [128, 512], F32)) for i in range(NPSF)]
    psb = [es.enter_context(nc.psum_tensor(f"psb{i}", [128, 1024], BF16)) for i in range(2)]
    st = {'ps': 0, 'pt': 0, 'ws': 0}

    def next_ps():
        i = st['ps'] % NPSF
        st['ps'] += 1
        return psf[i], ('psf', i)

    def next_pt():
        i = st['pt'] % 2
        st['pt'] += 1
        return psb[i][:, 0:512], ('psb', i)

    NSLOT = 3
    slots = [sb(f"ws{i}", [128, 8, 512], BF16) for i in range(NSLOT)]
    wsl = {'slots': slots, 'n': NSLOT, 'base': 0}

    class WK(tuple):
        pass

    NSCR = 40
    wscr = nc.dram_tensor("wscr", [NSCR, 128, 8 * 512], BF16).ap()
    wmap = {}
    wdump = {'on': False, 'use': False}

    def wkey(src):
        return (src.name, str(src.offset), tuple(src.shape))

    def wload(src):
        i = wsl['base'] + st['ws'] % wsl['n']
        st['ws'] += 1
        keys = [('ws', i, n_) for n_ in range(4)]
        sl_ = wsl['slots'][i - wsl['base']]
        k_ = wkey(src)
        if wdump['use'] and k_ in wmap:
            idx = wmap[k_]
            tr.dma('pool', sl_[:, :, :], wscr[idx].rearrange("p (k c) -> p k c", c=512), f"d_ws{i}", reads=[('wscr', idx)], writes=keys)
            return sl_, keys
        tr.dma('pool', sl_[:, :, :], src.rearrange("(k p) c -> p k c", p=128), f"d_ws{i}", writes=keys)
        if wdump['on'] and len(wmap) < NSCR and k_ not in wmap:
            idx = len(wmap)
            wmap[k_] = idx
            tr.dma('sp', wscr[idx].rearrange("p (k c) -> p k c", c=512), sl_[:, :, :], f"d_wd{i}", reads=keys, writes=[('wscr', idx)])
        return sl_, keys

    def wload3(srcs):
        i = st['ws'] % NSLOT
        st['ws'] += 1
        items = []
        for n_, src in enumerate(srcs):
            items.append((slots[i][:, :, n_ * 128:(n_ + 1) * 128], src.rearrange("(k p) c -> p k c", p=128), [('ws', i, n_)]))
        tr.dma_multi('pool', items, f"d_ws{i}")
        return slots[i], [('ws', i, n_) for n_ in range(4)]

    cp = sb("cp", [128, 8])
    sc = sb("sc", [128, 8])
    sc_rep = sb("sc_rep", [128, 8, 128], BF16)
    pf = sb("pf", [128, NPF])
    pr = sb("pr", [128, NPR])
    wdt = sb("wdt", [128, 8, 32], BF16)
    modT = sb("modT", [128, 48])
    A1 = sb("A1", [128, 8])
    A2 = sb("A2", [128, 8])
    g1bc = sb("g1bc", [128, D])
    g2bc = sb("g2bc", [128, D])
    scr = sb("scr", [128, 6, 515])
    cfull = scr[:, 0:2, :]
    cacc = scr[:, 2:4, 0:512]
    hvs = scr[:, 4:6, 0:512]
    ysb = scr[:, 0:2, 0:512]
    ytmp = scr[:, 2:4, 0:512]
    relu = scr[:, 4:6, 0:512]
    mtmp = scr[:, 0:2, 0:512]
    mtmp2 = scr[:, 2, 0:512]
    sto = scr[:, 0:2, 0:128]
    for i_ in range(2):
        tr.alias[('cfull', i_)] = ('scr', i_)
        tr.alias[('cfullp', i_)] = ('scrp', i_)
        tr.alias[('cacc', i_)] = ('scr', 2 + i_)
    for i_ in range(2):
        tr.alias[('hvs', i_)] = ('scr', 4 + i_)
        tr.alias[('ysb', i_)] = ('scr', i_)
        tr.alias[('ytmp', i_)] = ('scr', 2 + i_)
        tr.alias[('relu', i_)] = ('scr', 4 + i_)
        tr.alias[('mtmp', i_)] = ('scr', i_)
        tr.alias[('sto', i_)] = ('scr', i_)
    tr.alias[('mtmp2',)] = ('scr', 2)
    for i_ in range(4):
        tr.alias[('xnb', i_)] = ('X', 'n', i_)
        tr.alias[('xtm', 0, i_)] = ('X', 't', i_)
        tr.alias[('ynb', i_)] = ('X', 'y', i_)
    for i_ in range(8):
        tr.alias[('sgbT', i_)] = ('ainT', i_)
    epsb = sb("epsb", [128, 1])
    A('dve', lambda e: e.memset(epsb[:], EPS), writes=[('epsb',)])
    eabc = sb("eabc", [128, 32])
    tr.dma('sp', cp[:], cpT[:, :], 'd_cp', writes=[('cp',)])
    tr.dma('sp', pf[:], pfm[:, :], 'd_pf', writes=[('pf',)])
    tr.dma('sp', pr[:], prow[0:1, :].partition_broadcast(128), 'd_pr', writes=[('pr',)])
    tr.dma('sp', g1bc[:], bg12[0:1, :].partition_broadcast(128), 'd_g1', writes=[('g1bc',)])
    tr.dma('sp', g2bc[:], bg12[1:2, :].partition_broadcast(128), 'd_g2', writes=[('g2bc',)])
    tr.dma('pool', wdt[:], w_in[:, O_DT:O_DT + 32].rearrange("(k p) c -> p k c", p=128), 'd_wdt', writes=[('wdt',)])
    A('act', lambda e: e.activation(out=sc[:], in_=cp[:], func=AF.Silu), reads=[('cp',)], writes=[('sc',)])
    A('dve', lambda e: e.tensor_copy(out=sc_rep[:], in_=sc[:].unsqueeze(2).to_broadcast([128, 8, 128])),
      reads=[('sc',)], writes=[('sc_rep',)])
    A('act', lambda e: e.activation(out=eabc[:], in_=pr[:, R_ALOG:R_ALOG + 32], func=AF.Exp), reads=[('pr',)], writes=[('eabc',)])

    NB = NSB
    csT_d = din("csT", [128, 8 * NB])
    cs = sb("cs", [128, 8, NB])
    scs = sb("scs", [128, 8, NB], BF16)
    modS = sb("modS", [128, 48, NB])
    v2 = lambda t: t[:].rearrange("p a b -> p (a b)")
    tr.dma('sp', v2(cs), csT_d[:, :], 'd_cs', writes=[('cs',)])
    A('act', lambda e: e.activation(out=v2(scs), in_=v2(cs), func=AF.Silu), reads=[('cs',)], writes=[('scs',)])

    def bc3(ap2, n_):
        return ap2.unsqueeze(2).to_broadcast([128, n_, NB])

    tmT = scr[0:NB, 4:6, 0:512]
    st_tm = {'i': 0}

    def tm_to_fm(psA, kA, ps, pk):
        ti_ = st_tm['i'] % 2
        st_tm['i'] += 1
        A('act', lambda e: e.activation(out=tmT[:, ti_, :], in_=psA[0:NB, :], func=AF.Copy), reads=[kA], writes=[('scr', 4 + ti_)])
        for jj in range(4):
            A('pe', lambda e, jj=jj: e.transpose(ps[:, jj * NB:(jj + 1) * NB], tmT[:, ti_, jj * 128:(jj + 1) * 128], ident_f[0:NB, 0:NB]),
              reads=[('scr', 4 + ti_), ('ident_f',)], writes=[pk], inc=(jj == 3))

    def mm4(slot, wk, rhs_fn, rkeys, nk, ps, pk, lhs_slots=None):
        psA, kA = next_ps()
        for k in range(nk):
            sl_, wk_ = (slot, wk) if lhs_slots is None else lhs_slots[k // 8]
            A('pe', lambda e, k=k, sl_=sl_: e.matmul(psA[0:NB, :], lhsT=rhs_fn(k), rhs=sl_[:, k % 8, :], start=(k == 0), stop=(k == nk - 1)),
              reads=list(wk_) + rkeys, writes=[kA], inc=(k == nk - 1))
        tm_to_fm(psA, kA, ps, pk)

    psv = lambda ps: ps[:, 0:4 * NB].rearrange("p (a b) -> p a b", b=NB)

    for cg in range(12):
        slot, wk = wload(w_ada[:, cg * 512:(cg + 1) * 512])
        ps, pk = next_ps()
        for k in range(8):
            A('pe', lambda e, k=k: e.matmul(ps[:, :], lhsT=sc_rep[:, k, :], rhs=slot[:, k, :], start=(k == 0), stop=(k == 7)),
              reads=[('sc_rep',)] + wk, writes=[pk], inc=(k == 7))
        mi = cg % 2
        A('act', lambda e: e.activation(out=mtmp[:, mi, :], in_=ps[:, :], func=AF.Copy), reads=[pk], writes=[('mtmp', mi)])
        A('dve', lambda e: e.tensor_tensor(out=mtmp2[:].rearrange("p (a b) -> p a b", b=128),
                                           in0=mtmp[:, mi, :].rearrange("p (a b) -> p a b", b=128),
                                           in1=ident_f[:].unsqueeze(1).to_broadcast([128, 4, 128]), op=ALU.mult),
          reads=[('mtmp', mi), ('ident_f',)], writes=[('mtmp2',)])
        A('dve', lambda e: e.tensor_reduce(out=modT[:, 4 * cg:4 * cg + 4], in_=mtmp2[:].rearrange("p (a b) -> p a b", b=128),
                                           axis=AX.X, op=ALU.add), reads=[('mtmp2',)], writes=[('modT',)])
        if cg in (4, 5):
            o = (cg - 4) * 512
            A('dve', lambda e: e.tensor_tensor(out=g1bc[:, o:o + 512], in0=mtmp[:, mi, :], in1=g1bc[:, o:o + 512], op=ALU.add),
              reads=[('mtmp', mi), ('g1bc',)], writes=[('g1bc',)])
        if cg in (10, 11):
            o = (cg - 10) * 512
            A('dve', lambda e: e.tensor_tensor(out=g2bc[:, o:o + 512], in0=mtmp[:, mi, :], in1=g2bc[:, o:o + 512], op=ALU.add),
              reads=[('mtmp', mi), ('g2bc',)], writes=[('g2bc',)])
        ps_s, pk_s = next_ps()
        mm4(slot, wk, lambda k: scs[:, k, :], [('scs',)], 8, ps_s, pk_s)
        A('dve', lambda e: e.tensor_tensor(out=modS[:, 4 * cg:4 * cg + 4, :], in0=psv(ps_s), in1=bc3(pf[:, P_BADA + 4 * cg:P_BADA + 4 * cg + 4], 4), op=ALU.add),
          reads=[pk_s, ('pf',)], writes=[('modS', cg)])
    A('dve', lambda e: e.tensor_tensor(out=modT[:], in0=modT[:], in1=pf[:, P_BADA:P_BADA + 48], op=ALU.add),
      reads=[('modT',), ('pf',)], writes=[('modT',)])
    A('dve', lambda e: e.scalar_tensor_tensor(out=A1[:], in0=modT[:, 8:16], scalar=1.0, in1=pf[:, P_N1G:P_N1G + 8], op0=ALU.add, op1=ALU.mult),
      reads=[('modT',), ('pf',)], writes=[('A1',)])
    A('dve', lambda e: e.scalar_tensor_tensor(out=A2[:], in0=modT[:, 32:40], scalar=1.0, in1=pf[:, P_N2G:P_N2G + 8], op0=ALU.add, op1=ALU.mult),
      reads=[('modT',), ('pf',)], writes=[('A2',)])

    es2 = ExitStack()

    def sb2(n, s_, dt=F32):
        return es2.enter_context(nc.sbuf_tensor(n, list(s_), dt))

    bufs = [sb2("bufA", [128, 4, D]), sb2("bufB", [128, 4, D])]
    ss_n = sb2("ss_n", [128, 8])
    rstd_n = sb2("rstd_n", [128, 8])
    X = sb2("X", [128, 4096], BF16)
    xnb = X[:, :].rearrange("p (s f) -> p s f", f=D)
    xtm = X[:, 0:2048].rearrange("p (o f) -> p o f", o=1)
    ynb = X[:, 2048:4096]
    uT = sb2("uT", [128, 8, T], BF16)
    ainT = sb2("ainT", [128, 8, T], BF16)
    sgT = sb2("sgT", [128, 8, T], BF16)
    sgbT = ainT
    R1 = sb2("R1", [128, 16384], BF16)
    xcT = R1[:, 0:12288].rearrange("p (c t) -> p c t", t=T)
    Btm = R1[:, 12288:12800]
    xpp = R1[:, 14336:16384]
    hmT = R1[:, :].rearrange("p (c t) -> p c t", t=T)
    ynT = sb2("ynT", [128, 16, T], BF16)
    hT = sb2("hT", [128, 2048])
    hTb = sb2("hTb", [128, 2048], BF16)
    prevA = sb2("prevA", [128, 2, 8])
    prevB = sb2("prevB", [128, 3, 24])
    WT4 = sb2("WT4", [128, 2, 512], BF16)
    CBs = sb2("CBs", [128, 512])
    e4 = sb2("e4", [128, 2, 512], BF16)
    ddt4 = sb2("ddt4", [128, 4, 32])
    ss = sb2("ss", [128, 8])
    rstd = sb2("rstd", [128, 8])
    dtt = sb2("dtt", [128, 4, 32])
    dta = sb2("dta", [128, 4, 32])
    dtmp = sb2("dtmp", [128, 4, 32])
    dtmp2 = sb2("dtmp2", [128, 4, 32])
    acs4 = sb2("acs4", [128, 4, 32])
    nacs4 = sb2("nacs4", [128, 4, 32])
    eacs4 = sb2("eacs4", [128, 4, 32])
    d24 = sb2("d24", [128, 4, 32])
    ealb4 = sb2("ealb4", [128, 4, 32])
    acsT4 = sb2("acsT4", [32, 4, 2, 128], BF16)
    acsTf = sb2("acsTf", [32, 128])
    gss = sb2("gss", [128, 8])
    grs = sb2("grs", [128, 8])

    print('SBUF remaining after per-tile alloc (KB/partition):', nc.sbuf_bytes_remaining / 1024.0)
    A('dve', lambda e: e.memset(hT[:], 0.0), writes=[('hT', g) for g in range(4)])
    A('dve', lambda e: e.memset(hTb[:], 0.0), writes=[('hTb', g) for g in range(4)])
    A('dve', lambda e: e.memset(prevA[:], 0.0), writes=[('prevA', j) for j in range(8)])
    A('dve', lambda e: e.memset(prevB[:], 0.0), writes=[('prevB', c) for c in range(24)])

    def norm_to_T(A_sc, B_sc, Akey, Bkey, xb=None, xk=None, pre=False):
        xb = xt if xb is None else xb
        xk = (lambda s: ('xt', s)) if xk is None else xk
        rs_, rk_ = (rstd_n, ('rstd_n',)) if pre else (rstd, ('rstd',))
        if not pre:
            A('dve', lambda e: e.memset(ss[:, 0:4], 0.0), writes=[('ss',)])
            for s in range(4):
                A('act', lambda e, s=s: e.activation(out=xnb[:, s, :], in_=xb[:, s, :], func=AF.Square, accum_out=ss[:, s:s + 1]),
                  reads=[xk(s), ('ss',)], writes=[('xnb', s), ('ss',)])
            A('act', lambda e: e.activation(out=rstd[:, 4:8], in_=ss[:, 0:4], func=AF.Ln, scale=1.0 / D, bias=epsb[:, 0:1]),
              reads=[('ss',), ('epsb',)], writes=[('rstd_t',)])
            A('act', lambda e: e.activation(out=rstd[:, 0:4], in_=rstd[:, 4:8], func=AF.Exp, scale=-0.5), reads=[('rstd_t',)], writes=[('rstd',)])
        for s in range(4):
            A('dve', lambda e, s=s: e.tensor_scalar(out=xnb[:, s, :], in0=xb[:, s, :], scalar1=rs_[:, s:s + 1], scalar2=None, op0=ALU.mult),
              reads=[xk(s), rk_], writes=[('xnb', s)])
        for k in range(8):
            pt, ptk = next_pt()
            for s in range(4):
                A('pe', lambda e, s=s, k=k: e.transpose(pt[:, s * 128:(s + 1) * 128], xnb[:, s, k * 128:(k + 1) * 128], ident_b[:]),
                  reads=[('xnb', s), ('ident_b',)], writes=[ptk], inc=(s == 3))
            A('act', lambda e, k=k: e.activation(out=uT[:, k, :], in_=pt, func=AF.Identity, scale=A_sc[:, k:k + 1], bias=B_sc[:, k:k + 1]),
              reads=[ptk, Akey, Bkey], writes=[('uT', k)])

    def proj_fm(src_cols, evac):
        slot, wk = wload(src_cols)
        for jj in range(4):
            ps, pk = next_ps()
            for k in range(8):
                A('pe', lambda e, k=k, jj=jj: e.matmul(ps[:, :], lhsT=slot[:, k, jj * 128:(jj + 1) * 128], rhs=uT[:, k, :],
                                                         start=(k == 0), stop=(k == 7)),
                  reads=[wk[jj], ('uT', k)], writes=[pk], inc=(k == 7))
            evac(jj, ps, pk)

    for ti in range(NTILE):
        last = (ti == NTILE - 1)
        wdump['on'] = last
        t0 = ti * T
        P_, Q_ = ti % 2, (ti + 1) % 2
        Pn, Qn = f'B{P_}', f'B{Q_}'
        xt = bufs[P_]
        zs = bufs[Q_][:].bitcast(BF16)
        for s in range(4):
            tr.alias[('xt', s)] = (Pn, 'x', s)
            for cg in range(4):
                tr.alias[('zs', s, cg)] = (Qn, 'z', s, cg)
        tr.retire('R1')
        tr.retire('X')
        if ti == 0:
            for s in range(4):
                tr.dma('sp', xt[:, s, :], xp[t0 + s * 128:t0 + (s + 1) * 128, :], f'd_xt{s}', writes=[('xt', s)])
            norm_to_T(A1, modT[:, 0:8], ('A1',), ('modT',))
        else:
            tr.retire(Qn)

        ps, pk = next_ps()
        for s in range(4):
            for k in range(8):
                A('pe', lambda e, k=k, s=s: e.matmul(ps[:, s * 32:(s + 1) * 32], lhsT=uT[:, k, s * 128:(s + 1) * 128], rhs=wdt[:, k, :],
                                                      start=(k == 0), stop=(k == 7)),
                  reads=[('wdt',), ('uT', k)], writes=[pk], inc=(k == 7 and s == 3))
        dtv = lambda t: t[:].rearrange("p s h -> p (s h)")
        A('dve', lambda e: e.tensor_tensor(out=dtmp[:], in0=ps[:, 0:128].rearrange("p (s h) -> p s h", h=32),
                                           in1=pr[:, R_DTB:R_DTB + 32].unsqueeze(1).to_broadcast([128, 4, 32]), op=ALU.add),
          reads=[pk, ('pr',)], writes=[('dtmp',)])
        A('dve', lambda e: e.scalar_tensor_tensor(out=dtv(dtmp2), in0=dtv(dtmp), scalar=-1.0, in1=dtv(dtmp), op0=ALU.mult, op1=ALU.max),
          reads=[('dtmp',)], writes=[('dtmp2',)])
        A('act', lambda e: e.activation(out=dtv(dtmp2), in_=dtv(dtmp2), func=AF.Exp, scale=-1.0), reads=[('dtmp2',)], writes=[('dtmp2',)])
        A('act', lambda e: e.activation(out=dtv(dtmp2), in_=dtv(dtmp2), func=AF.Ln, bias=1.0), reads=[('dtmp2',)], writes=[('dtmp2',)])
        A('dve', lambda e: e.scalar_tensor_tensor(out=dtv(dtt), in0=dtv(dtmp), scalar=0.0, in1=dtv(dtmp2), op0=ALU.max, op1=ALU.add),
          reads=[('dtmp',), ('dtmp2',)], writes=[('dtt',)])
        A('dve', lambda e: e.scalar_tensor_tensor(out=dta[:], in0=dtt[:], scalar=-1.0, in1=eabc[:].unsqueeze(1).to_broadcast([128, 4, 32]),
                                                  op0=ALU.mult, op1=ALU.mult),
          reads=[('dtt',), ('eabc',)], writes=[('dta',)])

        def decay_block():
            for s in range(4):
                ps, pk = next_ps()
                A('pe', lambda e: e.matmul(ps[:, 0:32], lhsT=U[:], rhs=dta[:, s, :], start=True, stop=True), reads=[('U',), ('dta',)], writes=[pk], inc=False)
                A('pe', lambda e: e.matmul(ps[:, 32:64], lhsT=ones_f[:], rhs=dta[:, s, :], start=True, stop=True), reads=[('ones_f',), ('dta',)], writes=[pk], inc=False)
                A('pe', lambda e: e.matmul(ps[0:32, 128:256], lhsT=dta[:, s, :], rhs=U[:], start=True, stop=True), reads=[('U',), ('dta',)], writes=[pk])
                A('act', lambda e: e.activation(out=acsTf[:], in_=ps[0:32, 128:256], func=AF.Copy), reads=[pk], writes=[('acsTf',)])
                A('act', lambda e: e.activation(out=acs4[:, s, :], in_=ps[:, 0:32], func=AF.Copy), reads=[pk], writes=[('acs4', s)])
                A('act', lambda e: e.activation(out=nacs4[:, s, :], in_=ps[:, 0:32], func=AF.Copy, scale=-1.0), reads=[pk], writes=[('nacs4', s)])
                A('act', lambda e: e.activation(out=eacs4[:, s, :], in_=ps[:, 0:32], func=AF.Exp), reads=[pk], writes=[('eacs4', s)])
                A('act', lambda e: e.activation(out=ealb4[:, s, :], in_=ps[:, 32:64], func=AF.Exp), reads=[pk], writes=[('ealb4', s)])
                A('dve', lambda e: e.tensor_tensor(out=d24[:, s, :], in0=ps[:, 32:64], in1=acs4[:, s, :], op=ALU.subtract), reads=[pk, ('acs4', s)], writes=[('d24', s)])
                A('act', lambda e: e.activation(out=d24[:, s, :], in_=d24[:, s, :], func=AF.Exp), reads=[('d24', s)], writes=[('d24', s)])
                A('dve', lambda e: e.tensor_copy(out=acsT4[:, s, 0, :], in_=acsTf[:]), reads=[('acsTf',)], writes=[('acsT4', s, 0)])
                A('dve', lambda e: e.tensor_tensor(out=acsTf[:], in0=acsTf[:], in1=acsT4[:, s, 0, :], op=ALU.subtract),
                  reads=[('acsTf',), ('acsT4', s, 0)], writes=[('acsTf',)])
                A('dve', lambda e: e.tensor_copy(out=acsT4[:, s, 1, :], in_=acsTf[:]), reads=[('acsTf',)], writes=[('acsT4', s, 1)])
            A('dve', lambda e: e.reciprocal(out=dtv(ddt4), in_=dtv(dtt)), reads=[('dtt',)], writes=[('ddt4',)])
            A('dve', lambda e: e.tensor_tensor(out=ddt4[:], in0=ddt4[:], in1=pr[:, R_DSK:R_DSK + 32].unsqueeze(1).to_broadcast([128, 4, 32]), op=ALU.mult),
              reads=[('ddt4',), ('pr',)], writes=[('ddt4',)])


        for j in range(8):
            if True:
                slot, wk = wload3([w_in[:, o + j * 128:o + (j + 1) * 128] for o in (O_BG, O_CG, O_HV)])
                pss = []
                for n_ in range(3):
                    ps, pk = next_ps()
                    for k in range(8):
                        A('pe', lambda e, k=k, ps=ps: e.matmul(ps[:, :], lhsT=slot[:, k, n_ * 128:(n_ + 1) * 128], rhs=uT[:, k, :],
                                                                start=(k == 0), stop=(k == 7)),
                          reads=[wk[n_], ('uT', k)], writes=[pk], inc=(k == 7))
                    pss.append((ps, pk))
                (pbg, kbg), (pcg, kcg), (phv, khv) = pss
                hi = j % 2
                ci = j % 2
                A('act', lambda e: e.activation(out=hvs[:, hi, :], in_=phv[:, :], func=AF.Copy), reads=[khv], writes=[('hvs', hi)])
                A('act', lambda e: e.activation(out=cfull[:, ci, 0:2], in_=prevA[:, :, j], func=AF.Copy), reads=[('prevA', j)], writes=[('cfullp', ci)])
                A('dve', lambda e: e.tensor_tensor(out=cfull[:, ci, 2:514], in0=pcg[:, :], in1=hvs[:, hi, :], op=ALU.mult),
                  reads=[kcg, ('hvs', hi)], writes=[('cfull', ci)])
                A('dve', lambda e: e.tensor_scalar(out=cacc[:, ci, :], in0=cfull[:, ci, 0:512], scalar1=pf[:, P_CAW + j:P_CAW + j + 1], scalar2=None, op0=ALU.mult),
                  reads=[('cfull', ci), ('cfullp', ci), ('pf',)], writes=[('cacc', ci)])
                for tap in (1, 2):
                    A('dve', lambda e, tap=tap: e.scalar_tensor_tensor(out=cacc[:, ci, :], in0=cfull[:, ci, tap:tap + 512],
                                                                       scalar=pf[:, P_CAW + tap * 8 + j:P_CAW + tap * 8 + j + 1],
                                                                       in1=cacc[:, ci, :], op0=ALU.mult, op1=ALU.add),
                      reads=[('cfull', ci), ('cfullp', ci), ('cacc', ci)], writes=[('cacc', ci)])
                A('dve', lambda e: e.tensor_tensor(out=ainT[:, j, :], in0=pbg[:, :], in1=cacc[:, ci, :], op=ALU.mult),
                  reads=[kbg, ('cacc', ci)], writes=[('ainT', j)])
                A('act', lambda e: e.activation(out=prevA[:, :, j], in_=cfull[:, ci, 512:514], func=AF.Copy),
                  reads=[('cfull', ci)], writes=[('prevA', j)])
                if j == 1:
                    decay_block()

        def z_group(cg):
            slot, wk = wload(w_in[:, O_Z + cg * 512:O_Z + (cg + 1) * 512])
            for s in range(4):
                ps, pk = next_ps()
                for k in range(8):
                    A('pe', lambda e, k=k, s=s: e.matmul(ps[:, :], lhsT=uT[:, k, s * 128:(s + 1) * 128], rhs=slot[:, k, :],
                                                          start=(k == 0), stop=(k == 7)),
                      reads=wk + [('uT', k)], writes=[pk], inc=(k == 7))
                A('act', lambda e, s=s: e.activation(out=zs[:, s, cg * 512:(cg + 1) * 512], in_=ps[:, :], func=AF.Silu),
                  reads=[pk], writes=[('zs', s, cg)])

        xbc_tail = {'f': None}
        for q in range(6):
            def evac_xbc(jj, ps, pk, q=q):
                c = q * 4 + jj
                ci = c % 2
                A('act', lambda e: e.activation(out=cfull[:, ci, 3:515], in_=ps[:, :], func=AF.Copy), reads=[pk], writes=[('cfull', ci)])
                A('act', lambda e: e.activation(out=cfull[:, ci, 0:3], in_=prevB[:, :, c], func=AF.Copy), reads=[('prevB', c)], writes=[('cfullp', ci)])
                A('act', lambda e: e.activation(out=cacc[:, ci, :], in_=ps[:, :], func=AF.Identity, scale=pf[:, P_CBW + 72 + c:P_CBW + 72 + c + 1],
                                                bias=pf[:, P_CBB + c:P_CBB + c + 1]),
                  reads=[pk, ('pf',)], writes=[('cacc', ci)])
                for tap in (0, 1, 2):
                    A('dve', lambda e, tap=tap: e.scalar_tensor_tensor(out=cacc[:, ci, :], in0=cfull[:, ci, tap:tap + 512],
                                                                       scalar=pf[:, P_CBW + tap * 24 + c:P_CBW + tap * 24 + c + 1],
                                                                       in1=cacc[:, ci, :], op0=ALU.mult, op1=ALU.add),
                      reads=[('cfull', ci), ('cfullp', ci), ('cacc', ci)], writes=[('cacc', ci)])
                if xbc_tail['f'] is not None:
                    xbc_tail['f']()

                def tail(c=c, ci=ci):
                    A('act', lambda e: e.activation(out=xcT[:, c, :], in_=cacc[:, ci, :], func=AF.Silu), reads=[('cacc', ci)], writes=[('R1', 'xc', c)])
                    A('act', lambda e: e.activation(out=prevB[:, :, c], in_=cfull[:, ci, 512:515], func=AF.Copy),
                      reads=[('cfull', ci)], writes=[('prevB', c)])
                xbc_tail['f'] = tail
            proj_fm(w_in[:, O_XBC + q * 512:O_XBC + (q + 1) * 512], evac_xbc)
            if q < 4:
                z_group(q)
        xbc_tail['f']()
        xbc_tail['f'] = None

        fill_state = {'in_ssd': True}

        def fill_bank():
            return (psf[3], ('psf', 3)) if fill_state['in_ssd'] else next_ps()

        front_slots = [(w_in[:, O_GA + q * 512:O_GA + (q + 1) * 512], 'ga', q) for q in range(2)] + \
                      [(w_a[:, q * 512:(q + 1) * 512], 'wa', q) for q in range(2)] + \
                      [(w_in[:, O_GB + q * 512:O_GB + (q + 1) * 512], 'gb', q) for q in range(2)]
        loaded = {}

        def front_load(i_):
            if i_ < len(front_slots) and i_ not in loaded:
                loaded[i_] = wload(front_slots[i_][0])

        def gen_front():
            for i_, (src, kind, q) in enumerate(front_slots):
                front_load(i_)
                front_load(i_ + 1)
                slot, wk = loaded[i_]
                rhs_t, rkey = (ainT, 'ainT') if kind == 'wa' else (uT, 'uT')
                for jj in range(4):
                    m = q * 4 + jj
                    ps, pk = fill_bank()
                    for k in range(8):
                        A('pe', lambda e, k=k: e.matmul(ps[:, :], lhsT=slot[:, k, jj * 128:(jj + 1) * 128], rhs=rhs_t[:, k, :], start=(k == 0), stop=(k == 7)),
                          reads=[wk[jj], (rkey, k)], writes=[pk], inc=(k == 7))

                    def evac(ps=ps, pk=pk, m=m, kind=kind):
                        if kind == 'ga':
                            A('act', lambda e: e.activation(out=sgT[:, m, :], in_=ps[:, :], func=AF.Copy), reads=[pk], writes=[('sgT', m)])
                            if m == 7:
                                for h2 in range(4):
                                    A('act', lambda e, h2=h2: e.activation(out=sgT[:, 2 * h2:2 * h2 + 2, :], in_=sgT[:, 2 * h2:2 * h2 + 2, :], func=AF.Sigmoid),
                                      reads=[('sgT', 2 * h2), ('sgT', 2 * h2 + 1)], writes=[('sgT', 2 * h2), ('sgT', 2 * h2 + 1)])
                        elif kind == 'wa':
                            A('dve', lambda e: e.tensor_tensor(out=sgT[:, m, :], in0=ps[:, :], in1=sgT[:, m, :], op=ALU.mult), reads=[pk, ('sgT', m)], writes=[('sgT', m)])
                        else:
                            A('act', lambda e: e.activation(out=sgbT[:, m, :], in_=ps[:, :], func=AF.Copy), reads=[pk], writes=[('sgbT', m)])
                    yield evac

        filler = gen_front()
        front_load(0)
        fstate = {'evac': None}

        def run_filler():
            if fstate['evac'] is not None:
                fstate['evac']()
                fstate['evac'] = None
            fstate['evac'] = next(filler, None)


        tr.retire('X')
        bank = lambda i_: (psf[i_], ('psf', i_))
        v3 = lambda ap: ap.rearrange("p (h d) -> p h d", d=64)
        v4 = lambda ap: ap.rearrange("p (a b) -> p a b", b=128)
        for s in range(4):
            sl = slice(s * 128, (s + 1) * 128)
            pcb, kcb = bank(2)
            for g in range(4):
                A('pe', lambda e, g=g: e.matmul(pcb[:, g * 128:(g + 1) * 128], lhsT=xcT[:, 16 + g, sl], rhs=xcT[:, 20 + g, sl], start=True, stop=True),
                  reads=[('R1', 'xc', 16 + g), ('R1', 'xc', 20 + g)], writes=[kcb], inc=(g == 3))
            A('act', lambda e: e.activation(out=CBs[:], in_=pcb[:, :], func=AF.Copy), reads=[kcb], writes=[('CBs',)])

            def stage1(k):
                pbc, kbc = bank(4 + k % 2)
                A('pe', lambda e: e.matmul(pbc[:, :], lhsT=ident_b[:], rhs=NEGM[:], start=True, stop=False),
                  reads=[('ident_b',), ('NEGM',)], writes=[kbc], inc=False)
                for j in range(4):
                    h = 4 * k + j
                    cs_ = slice(j * 128, (j + 1) * 128)
                    A('pe', lambda e, h=h, cs_=cs_: e.matmul(pbc[:, cs_], lhsT=Esel[:, h * 128:(h + 1) * 128], rhs=acsT4[:, s, 0, :], start=False, stop=False),
                      reads=[('Esel',), ('acsT4', s, 0)], writes=[kbc], inc=False)
                    A('pe', lambda e, h=h, cs_=cs_, j=j: e.matmul(pbc[:, cs_], lhsT=Esel[:, h * 128:(h + 1) * 128], rhs=acsT4[:, s, 1, :], start=False, stop=(j == 3)),
                      reads=[('Esel',), ('acsT4', s, 1)], writes=[kbc], inc=(j == 3))
                return pbc, kbc

            def epilogue_steps(g):
                gs = slice(g * 512, (g + 1) * 512)
                gi = g % 2
                pyd, kyd = bank(g % 2)
                pyo, kyo = bank(2)
                pst, kst = bank(2)

                def stepA():
                    A('pe', lambda e: e.matmul(pyo[:, :], lhsT=xcT[:, 20 + g, sl], rhs=hTb[:, gs], start=True, stop=True),
                      reads=[('R1', 'xc', 20 + g), ('hTb', g)], writes=[kyo])
                    A('dve', lambda e: e.tensor_tensor(out=v3(ytmp[:, gi, :]), in0=v3(xtm[:, 0, gs]),
                                                       in1=ddt4[:, s, g * 8:(g + 1) * 8].unsqueeze(2).to_broadcast([128, 8, 64]), op=ALU.mult),
                      reads=[('xtm', 0, g), ('ddt4',)], writes=[('ytmp', gi)])
                    A('dve', lambda e: e.tensor_tensor(out=v3(ysb[:, gi, :]), in0=v3(pyo[:, :]),
                                                       in1=eacs4[:, s, g * 8:(g + 1) * 8].unsqueeze(2).to_broadcast([128, 8, 64]), op=ALU.mult),
                      reads=[kyo, ('eacs4', s)], writes=[('ysb', gi)])

                def stepB():
                    A('dve', lambda e: e.tensor_tensor(out=ysb[:, gi, :], in0=ysb[:, gi, :], in1=pyd[:, :], op=ALU.add),
                      reads=[kyd, ('ysb', gi)], writes=[('ysb', gi)])
                    A('dve', lambda e: e.tensor_tensor(out=ysb[:, gi, :], in0=ysb[:, gi, :], in1=ytmp[:, gi, :], op=ALU.add),
                      reads=[('ytmp', gi), ('ysb', gi)], writes=[('ysb', gi)])
                    A('pe', lambda e: e.matmul(pst[:, :], lhsT=Btm[:, g * 128:(g + 1) * 128], rhs=xpp[:, gs], start=True, stop=True),
                      reads=[('R1', 'btm'), ('R1', 'xpp')], writes=[kst])

                def stepC():
                    A('dve', lambda e: e.tensor_tensor(out=ysb[:, gi, :], in0=ysb[:, gi, :], in1=zs[:, s, gs], op=ALU.mult),
                      reads=[('zs', s, g), ('ysb', gi)], writes=[('ysb', gi)])
                    A('dve', lambda e: e.memset(gss[:, g:g + 1], 0.0), writes=[('gss', g)])
                    A('dve', lambda e: e.tensor_tensor(out=v3(hT[:, gs]), in0=v3(hT[:, gs]),
                                                       in1=ealb4[:, s, g * 8:(g + 1) * 8].unsqueeze(2).to_broadcast([128, 8, 64]), op=ALU.mult),
                      reads=[('hT', g), ('ealb4', s)], writes=[('hT', g)])
                    A('dve', lambda e: e.tensor_tensor(out=hT[:, gs], in0=hT[:, gs], in1=pst[:, :], op=ALU.add),
                      reads=[('hT', g), kst], writes=[('hT', g)])

                def stepD():
                    A('act', lambda e: e.activation(out=ytmp[:, gi, :], in_=ysb[:, gi, :], func=AF.Square, accum_out=gss[:, g:g + 1]),
                      reads=[('ysb', gi), ('gss', g)], writes=[('ytmp', gi), ('gss', g)])
                    A('act', lambda e: e.activation(out=gss[:, g:g + 1], in_=gss[:, g:g + 1], func=AF.Ln, scale=1.0 / 512, bias=epsb[:, 0:1]),
                      reads=[('gss', g), ('epsb',)], writes=[('gss', g)])
                    A('act', lambda e: e.activation(out=grs[:, g:g + 1], in_=gss[:, g:g + 1], func=AF.Exp, scale=-0.5), reads=[('gss', g)], writes=[('grs', g)])
                    A('act', lambda e: e.activation(out=ynb[:, gs], in_=ysb[:, gi, :], func=AF.Copy, scale=grs[:, g:g + 1]),
                      reads=[('ysb', gi), ('grs', g)], writes=[('ynb', g)])
                    A('act', lambda e: e.activation(out=hTb[:, gs], in_=hT[:, gs], func=AF.Copy), reads=[('hT', g)], writes=[('hTb', g)])

                return [stepA, stepB, stepC, stepD]

            pending = []
            lastD = {'f': None}
            pb = {}
            for k in range(2):
                pb[k] = stage1(k)
            for q in range(4):
                pt, ptk = next_pt()
                for jj in range(4):
                    c = q * 4 + jj
                    A('pe', lambda e, c=c, jj=jj: e.transpose(pt[:, jj * 128:(jj + 1) * 128], xcT[:, c, sl], ident_b[:]),
                      reads=[('R1', 'xc', c), ('ident_b',)], writes=[ptk], inc=(jj == 3))
                A('dve', lambda e, q=q: e.tensor_tensor(out=v3(xtm[:, 0, q * 512:(q + 1) * 512]), in0=v3(pt),
                                                        in1=dtt[:, s, q * 8:(q + 1) * 8].unsqueeze(2).to_broadcast([128, 8, 64]), op=ALU.mult),
                  reads=[ptk, ('dtt',)], writes=[('xtm', 0, q)])
            pt, ptk = next_pt()
            for g in range(4):
                A('pe', lambda e, g=g: e.transpose(pt[:, g * 128:(g + 1) * 128], xcT[:, 16 + g, sl], ident_b[:]),
                  reads=[('R1', 'xc', 16 + g), ('ident_b',)], writes=[ptk], inc=(g == 3))
            A('act', lambda e: e.activation(out=Btm, in_=pt, func=AF.Copy), reads=[ptk], writes=[('R1', 'btm')])
            for k in range(8):
                g = k // 2
                ei = k % 2
                pbc, kbc = pb[k]
                for j in range(4):
                    h = 4 * k + j
                    A('act', lambda e, h=h, j=j: e.activation(out=e4[:, ei, j * 128:(j + 1) * 128], in_=pbc[:, j * 128:(j + 1) * 128], func=AF.Exp,
                                                              bias=nacs4[:, s, h:h + 1]),
                      reads=[kbc, ('nacs4', s)], writes=[('e4', ei, j)])
                A('dve', lambda e: e.tensor_tensor(out=v4(WT4[:, ei, :]), in0=v4(e4[:, ei, :]),
                                                   in1=CBs[:, g * 128:(g + 1) * 128].unsqueeze(1).to_broadcast([128, 4, 128]), op=ALU.mult),
                  reads=[('e4', ei, j_) for j_ in range(4)] + [('CBs',)], writes=[('WT4', ei)])
                pyd, kyd = bank(g % 2)
                for j in range(4):
                    h = 4 * k + j
                    r = h % 8
                    A('pe', lambda e, h=h, r=r, j=j: e.matmul(pyd[:, r * 64:(r + 1) * 64], lhsT=WT4[:, ei, j * 128:(j + 1) * 128], rhs=xtm[:, 0, h * 64:(h + 1) * 64],
                                                               start=True, stop=True),
                      reads=[('WT4', ei), ('xtm', 0, g)], writes=[kyd], inc=(j == 3))
                if k + 2 < 8:
                    pb[k + 2] = stage1(k + 2)
                if k == 0:
                    A('dve', lambda e: e.tensor_tensor(out=v3(xpp), in0=v3(xtm[:, 0, :]), in1=d24[:, s, :].unsqueeze(2).to_broadcast([128, 32, 64]), op=ALU.mult),
                      reads=[('xtm', 0, q) for q in range(4)] + [('d24', s)], writes=[('R1', 'xpp')])
                if k % 2 == 1:
                    sA, sB, sC, sD = epilogue_steps(g)
                    pending.extend([sA, sB, sC] + ([lastD['f']] if lastD['f'] is not None else []))
                    lastD['f'] = sD
                for _ in range(2):
                    if pending and k >= 2:
                        pending.pop(0)()
                run_filler()
            while pending:
                pending.pop(0)()
            lastD['f']()
            lastD['f'] = None
            for q in range(4):
                pt, ptk = next_pt()
                for jj in range(4):
                    c = q * 4 + jj
                    A('pe', lambda e, c=c, jj=jj: e.transpose(pt[:, jj * 128:(jj + 1) * 128], ynb[:, c * 128:(c + 1) * 128], ident_b[:]),
                      reads=[('ynb', q), ('ident_b',)], writes=[ptk], inc=(jj == 3))
                A('dve', lambda e, q=q: e.tensor_tensor(out=ynT[:, q * 4:(q + 1) * 4, sl], in0=pt.rearrange("p (a b) -> p a b", b=128),
                                                        in1=pf[:, P_SNG + q * 4:P_SNG + (q + 1) * 4].unsqueeze(2).to_broadcast([128, 4, 128]), op=ALU.mult),
                  reads=[ptk, ('pf',)], writes=[('ynT', q, s)])

        tail_jobs = []
        if last:
            def job_a():
                ps, pk = next_ps()
                A('pe', lambda e: e.transpose(ps[0:16, 0:128], prevA[:].rearrange("p r j -> p (r j)"), ident_f[:]),
                  reads=[('prevA', j) for j in range(8)] + [('ident_f',)], writes=[pk])
                A('act', lambda e: e.activation(out=sto[0:16, 0, :], in_=ps[0:16, 0:128], func=AF.Copy), reads=[pk], writes=[('sto', 0)])
                tr.dma('sp', na_p.rearrange("r (j p) -> (r j) p", p=128), sto[0:16, 0, :], 'd_sto0', reads=[('sto', 0)])

            def job_b():
                ps, pk = next_ps()
                A('pe', lambda e: e.transpose(ps[0:72, 0:128], prevB[:].rearrange("p r c -> p (r c)"), ident_f[:]),
                  reads=[('prevB', c) for c in range(24)] + [('ident_f',)], writes=[pk])
                A('act', lambda e: e.activation(out=sto[0:72, 1, :], in_=ps[0:72, 0:128], func=AF.Copy), reads=[pk], writes=[('sto', 1)])
                tr.dma('sp', nbc_p.rearrange("r (c p) -> (r c) p", p=128), sto[0:72, 1, :], 'd_sto1', reads=[('sto', 1)])

            def job_h(c):
                def f():
                    ps, pk = next_ps()
                    si = c % 2
                    A('pe', lambda e: e.transpose(ps[:, 0:128], hT[:, c * 128:(c + 1) * 128], ident_f[:]), reads=[('hT', c // 4), ('ident_f',)], writes=[pk])
                    A('act', lambda e: e.activation(out=sto[:, si, :], in_=ps[:, 0:128], func=AF.Copy), reads=[pk], writes=[('sto', si)])
                    tr.dma('sp', nssm_p[c * 128:(c + 1) * 128, :], sto[:, si, :], f'd_sto{si}', reads=[('sto', si)])
                return f
            tail_jobs = [job_a, job_b] + [job_h(c) for c in range(16)]

        def pop_tail():
            if tail_jobs:
                tail_jobs.pop(0)()

        if fstate['evac'] is not None:
            fstate['evac']()
            fstate['evac'] = None
        fill_state['in_ssd'] = False
        for ev in filler:
            ev()
        for h2 in range(4):
            A('act', lambda e, h2=h2: e.activation(out=sgbT[:, 2 * h2:2 * h2 + 2, :], in_=sgbT[:, 2 * h2:2 * h2 + 2, :], func=AF.Sigmoid),
              reads=[('sgbT', 2 * h2), ('sgbT', 2 * h2 + 1)], writes=[('sgbT', 2 * h2), ('sgbT', 2 * h2 + 1)])
        if not last:
            tr.retire(Qn)
            t0n = (ti + 1) * T
            for s in range(4):
                tr.dma('sp', bufs[Q_][:, s, :], xp[t0n + s * 128:t0n + (s + 1) * 128, :], f'd_xt{s}', writes=[(Qn, 'x', s)])
        for q in range(2):
            s0 = wload(w_b[0:1024, q * 512:(q + 1) * 512])
            s1 = wload(w_b[1024:2048, q * 512:(q + 1) * 512])
            for jj in range(4):
                m = q * 4 + jj
                ps, pk = next_ps()
                for kk in range(16):
                    slot, wk = (s0, s1)[kk // 8]
                    A('pe', lambda e, kk=kk, slot=slot, jj=jj: e.matmul(ps[:, :], lhsT=slot[:, kk % 8, jj * 128:(jj + 1) * 128], rhs=ynT[:, kk, :],
                                                                         start=(kk == 0), stop=(kk == 15)),
                      reads=[wk[jj]] + [('ynT', kk // 4, s) for s in range(4)], writes=[pk], inc=(kk == 15))
                hi = m % 2
                A('dve', lambda e, m=m: e.tensor_tensor(out=hvs[:, hi, :], in0=ps[:, :], in1=sgbT[:, m, :], op=ALU.mult), reads=[pk, ('sgbT', m)], writes=[('hvs', hi)])
                A('dve', lambda e, m=m: e.tensor_tensor(out=sgT[:, m, :], in0=sgT[:, m, :], in1=hvs[:, hi, :], op=ALU.add),
                  reads=[('hvs', hi), ('sgT', m)], writes=[('sgT', m)])
                pop_tail()
                pop_tail()
        while tail_jobs:
            pop_tail()
        for q in range(2):
            slot, wk = wload(w_o[:, q * 512:(q + 1) * 512])
            for s in range(4):
                ps, pk = next_ps()
                for k in range(8):
                    A('pe', lambda e, k=k, s=s: e.matmul(ps[:, :], lhsT=sgT[:, k, s * 128:(s + 1) * 128], rhs=slot[:, k, :], start=(k == 0), stop=(k == 7)),
                      reads=wk + [('sgT', k)], writes=[pk], inc=(k == 7))
                hi = (q * 4 + s) % 2
                A('dve', lambda e, q=q: e.tensor_tensor(out=hvs[:, hi, :], in0=ps[:, :], in1=g1bc[:, q * 512:(q + 1) * 512], op=ALU.mult),
                  reads=[pk, ('g1bc',)], writes=[('hvs', hi)])
                A('dve', lambda e, q=q, s=s: e.tensor_tensor(out=xt[:, s, q * 512:(q + 1) * 512], in0=xt[:, s, q * 512:(q + 1) * 512], in1=hvs[:, hi, :], op=ALU.add),
                  reads=[('hvs', hi), ('xt', s)], writes=[('xt', s)])

        tr.retire('R1')
        tr.retire('X')
        norm_to_T(A2, modT[:, 24:32], ('A2',), ('modT',))
        for q in range(8):
            def evac_h(jj, ps, pk, q=q):
                c = q * 4 + jj
                ri = c % 2
                A('act', lambda e: e.activation(out=relu[:, ri, :], in_=ps[:, :], func=AF.Relu), reads=[pk], writes=[('relu', ri)])
                A('dve', lambda e: e.tensor_tensor(out=hmT[:, c, :], in0=relu[:, ri, :], in1=relu[:, ri, :], op=ALU.mult),
                  reads=[('relu', ri)], writes=[('R1', 'hm', c)])
            proj_fm(w1[:, q * 512:(q + 1) * 512], evac_h)
            if q == 3 and not last:
                A('dve', lambda e: e.memset(ss_n[:, 0:4], 0.0), writes=[('ss_n',)])
                for s in range(4):
                    A('act', lambda e, s=s: e.activation(out=xnb[:, s, :], in_=bufs[Q_][:, s, :], func=AF.Square, accum_out=ss_n[:, s:s + 1]),
                      reads=[(Qn, 'x', s), ('ss_n',)], writes=[('xnb', s), ('ss_n',)])
                A('act', lambda e: e.activation(out=rstd_n[:, 4:8], in_=ss_n[:, 0:4], func=AF.Ln, scale=1.0 / D, bias=epsb[:, 0:1]),
                  reads=[('ss_n',), ('epsb',)], writes=[('rstd_nt',)])
                A('act', lambda e: e.activation(out=rstd_n[:, 0:4], in_=rstd_n[:, 4:8], func=AF.Exp, scale=-0.5), reads=[('rstd_nt',)], writes=[('rstd_n',)])
        for q in range(2):
            pss = [next_ps() for _ in range(4)]
            for kg in range(4):
                slot, wk = wload(w2[kg * 1024:(kg + 1) * 1024, q * 512:(q + 1) * 512])
                for s in range(4):
                    ps, pk = pss[s]
                    for k in range(8):
                        c = kg * 8 + k
                        A('pe', lambda e, k=k, s=s, c=c, ps=ps: e.matmul(ps[:, :], lhsT=hmT[:, c, s * 128:(s + 1) * 128], rhs=slot[:, k, :],
                                                                          start=(c == 0), stop=(c == 31)),
                          reads=wk + [('R1', 'hm', c)], writes=[pk], inc=(k == 7))
            for s in range(4):
                ps, pk = pss[s]
                hi = (q * 4 + s) % 2
                A('dve', lambda e, q=q, ps=ps: e.tensor_tensor(out=hvs[:, hi, :], in0=ps[:, :], in1=g2bc[:, q * 512:(q + 1) * 512], op=ALU.mult),
                  reads=[pk, ('g2bc',)], writes=[('hvs', hi)])
                A('dve', lambda e, q=q, s=s: e.tensor_tensor(out=xt[:, s, q * 512:(q + 1) * 512], in0=xt[:, s, q * 512:(q + 1) * 512], in1=hvs[:, hi, :], op=ALU.add),
                  reads=[('hvs', hi), ('xt', s)], writes=[('xt', s)])

        if not last:
            norm_to_T(A1, modT[:, 0:8], ('A1',), ('modT',), xb=bufs[Q_], xk=lambda s: (Qn, 'x', s), pre=True)

        A('dve', lambda e: e.memset(ss[:, 0:4], 0.0), writes=[('ss',)])
        for s in range(4):
            A('act', lambda e, s=s: e.activation(out=xnb[:, s, :], in_=xt[:, s, :], func=AF.Square, accum_out=ss[:, s:s + 1]),
              reads=[('xt', s), ('ss',)], writes=[('xnb', s), ('ss',)])
        A('act', lambda e: e.activation(out=rstd[:, 4:8], in_=ss[:, 0:4], func=AF.Ln, scale=1.0 / D, bias=epsb[:, 0:1]),
          reads=[('ss',), ('epsb',)], writes=[('rstd_t',)])
        A('act', lambda e: e.activation(out=rstd[:, 0:4], in_=rstd[:, 4:8], func=AF.Exp, scale=-0.5), reads=[('rstd_t',)], writes=[('rstd',)])
        for s in range(4):
            A('dve', lambda e, s=s: e.scalar_tensor_tensor(out=xt[:, s, :], in0=xt[:, s, :], scalar=rstd[:, s:s + 1], in1=pr[:, R_NFG:R_NFG + D],
                                                           op0=ALU.mult, op1=ALU.mult),
              reads=[('xt', s), ('rstd',), ('pr',)], writes=[('xt', s)])
            tr.dma('sp', yp[t0 + s * 128:t0 + (s + 1) * 128, :], xt[:, s, :], f'd_yp{s}', reads=[('xt', s)])

    wdump['on'] = False
    wdump['use'] = True
    tr.barrier()
    es2.close()
    es3 = ExitStack()

    def sb3(n, s_, dt=F32):
        return es3.enter_context(nc.sbuf_tensor(n, list(s_), dt))

    NSL2 = 8
    wsl['slots'] = [sb3(f"wss{i}", [128, 8, 512], BF16) for i in range(NSL2)]
    wsl['n'] = NSL2
    wsl['base'] = 3
    xsT_d = din("xsT", [128, 8 * NB])
    stAT_d = din("stAT", [128, 8 * 2 * NB])
    stBT_d = din("stBT", [128, 24 * 3 * NB])
    sssm_d = din("sssm", [NB, 2048, 128])
    ysT_d = dout("ysT", [128, 8 * NB])
    nasT_d = dout("nasT", [128, 8 * 2 * NB])
    nbsT_d = dout("nbsT", [128, 24 * 3 * NB])
    nssm_s_d = dout("nssm_s", [NB, 2048, 128])

    xs = sb3("xs", [128, 8, NB])
    stA = sb3("stA", [128, 8, 2, NB])
    stB = sb3("stB", [128, 24, 3, NB])
    nas = sb3("nas", [128, 8, 2, NB])
    nbs = sb3("nbs", [128, 24, 3, NB])
    A1s = sb3("A1s", [128, 8, NB])
    A2s = sb3("A2s", [128, 8, NB])
    sq = sb3("sq", [128, 16, NB])
    rs = sb3("rs", [128, NB])
    t8 = sb3("t8", [128, 8, NB])
    uTs = sb3("uTs", [128, 8, NB], BF16)
    projT = sb3("projT", [128, 81, NB])
    t24 = sb3("t24", [128, 24, NB])
    c24 = sb3("c24", [128, 24, NB])
    xcs = sb3("xcs", [128, 24, NB])
    ains = sb3("ains", [128, 8, NB], BF16)
    zss = sb3("zss", [128, 16, NB])
    dts = sb3("dts", [32, 33])
    dtm = sb3("dtm", [32, 2, NB])
    eacol = sb3("eacol", [32, 1])
    hl = sb3("hl", [32, 2, 33], BF16)
    hpT = sb3("hpT", [128, 16, 32])
    decP = sb3("decP", [128, 16, NB])
    dskP = sb3("dskP", [128, 16])
    xdt = sb3("xdt", [128, 16, NB])
    BCtm = sb3("BCtm", [16, 1024])
    BChl = sb3("BChl", [16, 2, 1024], BF16)
    hbuf = [sb3(f"hbuf{i}", [128, 16, 128]) for i in range(2)]
    t1b = sb3("t1b", [128, 16, 128])
    junk = sb3("junk", [128, 4, 128])
    yS = sb3("yS", [128, 16, NB])
    t16 = sb3("t16", [128, 16, NB])
    rg = sb3("rg", [128, 4, NB])
    ynTs = sb3("ynTs", [128, 16, NB], BF16)
    sga = sb3("sga", [128, 8, NB])
    sgb = sb3("sgb", [128, 8, NB])
    ma = sb3("ma", [128, 8, NB])
    ms = sb3("ms", [128, 8, NB], BF16)
    r4 = sb3("r4", [128, 4, NB])
    hms = sb3("hms", [128, 32, NB], BF16)

    tr.dma('sp', v2(xs), xsT_d[:, :], 'd_xs', writes=[('xs',)])
    tr.dma('sp', stA[:].rearrange("p a b c -> p (a b c)"), stAT_d[:, :], 'd_stA', writes=[('stA',)])
    tr.dma('sp', stB[:].rearrange("p a b c -> p (a b c)"), stBT_d[:, :], 'd_stB', writes=[('stB',)])
    MS = [('modS', cg) for cg in range(12)]
    A('dve', lambda e: e.scalar_tensor_tensor(out=A1s[:], in0=modS[:, 8:16, :], scalar=1.0, in1=bc3(pf[:, P_N1G:P_N1G + 8], 8), op0=ALU.add, op1=ALU.mult),
      reads=MS + [('pf',)], writes=[('A1s',)])
    A('dve', lambda e: e.scalar_tensor_tensor(out=A2s[:], in0=modS[:, 32:40, :], scalar=1.0, in1=bc3(pf[:, P_N2G:P_N2G + 8], 8), op0=ALU.add, op1=ALU.mult),
      reads=MS + [('pf',)], writes=[('A2s',)])

    def rms_s():
        A('dve', lambda e: e.tensor_tensor(out=sq[:, 0:8, :], in0=xs[:], in1=xs[:], op=ALU.mult), reads=[('xs',)], writes=[('sq',)])
        ps, pk = next_ps()
        for k in range(8):
            A('pe', lambda e, k=k: e.matmul(ps[:, 0:NB], lhsT=ones_f[:], rhs=sq[:, k, :], start=(k == 0), stop=(k == 7)),
              reads=[('ones_f',), ('sq',)], writes=[pk], inc=(k == 7))
        A('act', lambda e: e.activation(out=rs[:], in_=ps[:, 0:NB], func=AF.Ln, scale=1.0 / D, bias=epsb[:, 0:1]), reads=[pk, ('epsb',)], writes=[('rs',)])
        A('act', lambda e: e.activation(out=rs[:], in_=rs[:], func=AF.Exp, scale=-0.5), reads=[('rs',)], writes=[('rs',)])

    def mod_norm(As, Akey, sh0):
        rms_s()
        A('dve', lambda e: e.tensor_tensor(out=t8[:], in0=xs[:], in1=rs[:].unsqueeze(1).to_broadcast([128, 8, NB]), op=ALU.mult),
          reads=[('xs',), ('rs',)], writes=[('t8',)])
        A('dve', lambda e: e.tensor_tensor(out=t8[:], in0=t8[:], in1=As[:], op=ALU.mult), reads=[('t8',), Akey], writes=[('t8',)])
        A('dve', lambda e: e.tensor_tensor(out=uTs[:], in0=t8[:], in1=modS[:, sh0:sh0 + 8, :], op=ALU.add), reads=[('t8',)] + MS, writes=[('uTs',)])

    mod_norm(A1s, ('A1s',), 0)

    def proj_s(cols, idx0):
        slot, wk = wload(cols)
        ps, pk = next_ps()
        mm4(slot, wk, lambda k: uTs[:, k, :], [('uTs',)], 8, ps, pk)
        A('act', lambda e: e.activation(out=projT[:, idx0:idx0 + 4, :], in_=psv(ps), func=AF.Copy), reads=[pk], writes=[('projT', idx0 // 4 if idx0 < 64 else idx0)])

    for off in list(range(O_XBC, 8192, 512)) + list(range(O_Z, O_XBC, 512)):
        proj_s(w_in[:, off:off + 512], off // 128)
    ps, pk = next_ps()
    for k in range(8):
        A('pe', lambda e, k=k: e.matmul(ps[0:32, 0:NB], lhsT=wdt[:, k, :], rhs=uTs[:, k, :], start=(k == 0), stop=(k == 7)),
          reads=[('wdt',), ('uTs',)], writes=[pk], inc=(k == 7))
    late_proj = [(w_in[:, off:off + 512], off // 128) for off in range(0, O_Z, 512)]
    for q in range(2):
        late_proj.append((w_in[:, O_GA + q * 512:O_GA + (q + 1) * 512], 65 + 4 * q))
        late_proj.append((w_in[:, O_GB + q * 512:O_GB + (q + 1) * 512], 73 + 4 * q))
    PJ = [('projT', i_) for i_ in range(16)] + [('projT', 65), ('projT', 69), ('projT', 73), ('projT', 77)]

    wB = lambda tap: bc3(pf[:, P_CBW + tap * 24:P_CBW + tap * 24 + 24], 24)
    xbc = projT[:, 40:64, :]
    A('dve', lambda e: e.tensor_tensor(out=c24[:], in0=stB[:, :, 0, :], in1=wB(0), op=ALU.mult), reads=[('stB',), ('pf',), ('ains',)], writes=[('c24',)])
    for tap, src in ((1, stB[:, :, 1, :]), (2, stB[:, :, 2, :]), (3, xbc)):
        A('dve', lambda e, tap=tap, src=src: e.tensor_tensor(out=t24[:], in0=src, in1=wB(tap), op=ALU.mult), reads=[('stB',), ('pf',), ('nas', 1)] + PJ, writes=[('t24',)])
        A('dve', lambda e: e.tensor_tensor(out=c24[:], in0=c24[:], in1=t24[:], op=ALU.add), reads=[('c24',), ('t24',)], writes=[('c24',)])
    A('dve', lambda e: e.tensor_tensor(out=c24[:], in0=c24[:], in1=bc3(pf[:, P_CBB:P_CBB + 24], 24), op=ALU.add), reads=[('c24',), ('pf',)], writes=[('c24',)])
    A('act', lambda e: e.activation(out=xcs[:], in_=c24[:], func=AF.Silu), reads=[('c24',)], writes=[('xcs',)])
    A('act', lambda e: e.activation(out=nbs[:, :, 0, :], in_=stB[:, :, 1, :], func=AF.Copy), reads=[('stB',)], writes=[('nbs', 0)])
    A('act', lambda e: e.activation(out=nbs[:, :, 1, :], in_=stB[:, :, 2, :], func=AF.Copy), reads=[('stB',)], writes=[('nbs', 1)])
    A('act', lambda e: e.activation(out=nbs[:, :, 2, :], in_=xbc, func=AF.Copy), reads=PJ, writes=[('nbs', 2)])
    tr.dma('sp', nbsT_d[:, :], nbs[:].rearrange("p a b c -> p (a b c)"), 'd_nbs', reads=[('nbs', 0), ('nbs', 1), ('nbs', 2)])
    A('act', lambda e: e.activation(out=zss[:], in_=projT[:, 24:40, :], func=AF.Silu), reads=PJ, writes=[('zss',)])

    A('dve', lambda e: e.tensor_scalar(out=dtm[:, 0, :], in0=ps[0:32, 0:NB], scalar1=pf[0:32, P_HP:P_HP + 1], scalar2=None, op0=ALU.add),
      reads=[pk, ('pf',)], writes=[('dtm',)])
    A('dve', lambda e: e.scalar_tensor_tensor(out=dtm[:, 1, :], in0=dtm[:, 0, :], scalar=-1.0, in1=dtm[:, 0, :], op0=ALU.mult, op1=ALU.max),
      reads=[('dtm',)], writes=[('dtm1',)])
    A('act', lambda e: e.activation(out=dtm[:, 1, :], in_=dtm[:, 1, :], func=AF.Exp, scale=-1.0), reads=[('dtm1',)], writes=[('dtm1',)])
    A('act', lambda e: e.activation(out=dtm[:, 1, :], in_=dtm[:, 1, :], func=AF.Ln, bias=1.0), reads=[('dtm1',)], writes=[('dtm1',)])
    A('dve', lambda e: e.scalar_tensor_tensor(out=dts[:, 0:NB], in0=dtm[:, 0, :], scalar=0.0, in1=dtm[:, 1, :], op0=ALU.max, op1=ALU.add),
      reads=[('dtm',), ('dtm1',)], writes=[('dts',)])
    A('act', lambda e: e.activation(out=eacol[:], in_=pf[0:32, P_HP + 1:P_HP + 2], func=AF.Exp), reads=[('pf',)], writes=[('eacol',)])
    A('dve', lambda e: e.tensor_scalar(out=dts[:, NB:2 * NB], in0=dts[:, 0:NB], scalar1=eacol[:, 0:1], scalar2=-1.0, op0=ALU.mult, op1=ALU.mult),
      reads=[('dts',), ('eacol',)], writes=[('dts',)])
    A('dve', lambda e: e.tensor_copy(out=dts[:, 32:33], in_=pf[0:32, P_HP + 2:P_HP + 3]), reads=[('pf',), ('dts',)], writes=[('dts',)])
    A('dve', lambda e: e.tensor_copy(out=hl[:, 0, :], in_=dts[:]), reads=[('dts',)], writes=[('hl', 0)])
    A('dve', lambda e: e.tensor_tensor(out=dts[:], in0=dts[:], in1=hl[:, 0, :], op=ALU.subtract), reads=[('dts',), ('hl', 0)], writes=[('dts',)])
    A('dve', lambda e: e.tensor_copy(out=hl[:, 1, :], in_=dts[:]), reads=[('dts',)], writes=[('hl', 1)])
    ps1, pk1 = next_ps()
    ps2, pk2 = next_ps()
    for c in range(16):
        lh = Esel[:, 2 * c * 128 + 64:2 * c * 128 + 192]
        for i_ in range(2):
            A('pe', lambda e, c=c, i_=i_, lh=lh: e.matmul(ps1[:, c * 32:(c + 1) * 32], lhsT=lh, rhs=hl[:, i_, 0:32], start=(i_ == 0), stop=(i_ == 1)),
              reads=[('Esel',), ('hl', i_)], writes=[pk1], inc=False)
        for i_ in range(2):
            A('pe', lambda e, c=c, i_=i_, lh=lh: e.matmul(ps2[:, c:c + 1], lhsT=lh, rhs=hl[:, i_, 32:33], start=(i_ == 0), stop=(i_ == 1)),
              reads=[('Esel',), ('hl', i_)], writes=[pk2], inc=(i_ == 1 and c == 15))
    A('act', lambda e: e.activation(out=hpT[:].rearrange("p a b -> p (a b)"), in_=ps1[:, :], func=AF.Copy), reads=[pk1, pk2], writes=[('hpT',)])
    A('act', lambda e: e.activation(out=dskP[:], in_=ps2[:, 0:16], func=AF.Copy), reads=[pk2], writes=[('dskP',)])
    A('act', lambda e: e.activation(out=decP[:], in_=hpT[:, :, NB:2 * NB], func=AF.Exp), reads=[('hpT',)], writes=[('decP',)])
    A('dve', lambda e: e.tensor_tensor(out=xdt[:], in0=xcs[:, 0:16, :], in1=hpT[:, :, 0:NB], op=ALU.mult), reads=[('xcs',), ('hpT',)], writes=[('xdt',)])
    for half in range(2):
        ps, pk = next_ps()
        for i_ in range(4):
            A('pe', lambda e, i_=i_: e.transpose(ps[0:NB, i_ * 128:(i_ + 1) * 128], xcs[:, 16 + half * 4 + i_, :], ident_f[:]),
              reads=[('xcs',), ('ident_f',)], writes=[pk], inc=(i_ == 3))
        A('act', lambda e: e.activation(out=BCtm[:, half * 512:(half + 1) * 512], in_=ps[0:NB, :], func=AF.Copy), reads=[pk], writes=[('BCtm', half)])
    A('dve', lambda e: e.tensor_copy(out=BChl[:, 0, :], in_=BCtm[:]), reads=[('BCtm', 0), ('BCtm', 1)], writes=[('BChl', 0)])
    A('dve', lambda e: e.tensor_tensor(out=BCtm[:], in0=BCtm[:], in1=BChl[:, 0, :], op=ALU.subtract), reads=[('BCtm', 0), ('BCtm', 1), ('BChl', 0)], writes=[('BCtm', 0), ('BCtm', 1)])
    A('dve', lambda e: e.tensor_copy(out=BChl[:, 1, :], in_=BCtm[:]), reads=[('BCtm', 0), ('BCtm', 1)], writes=[('BChl', 1)])
    A('dve', lambda e: e.memset(yS[:], 0.0), writes=[('yS', c_) for c_ in range(16)])

    for b in range(NB):
        bi = b % 2
        hb = hbuf[bi]
        if late_proj:
            proj_s(*late_proj.pop(0))
        if b == 0:
            tr.dma('sp', hb[:], sssm_d[0].rearrange("(c q) n -> q c n", q=128), f'd_h{bi}', writes=[('hbuf', bi, c_) for c_ in range(16)])
        if b + 1 < NB:
            tr.dma('sp', hbuf[1 - bi][:], sssm_d[b + 1].rearrange("(c q) n -> q c n", q=128), f'd_h{1 - bi}',
                   writes=[('hbuf', 1 - bi, c_) for c_ in range(16)])
        pB, kB = psf[bi], ('psf', bi)
        pC, kC = psf[2 + bi], ('psf', 2 + bi)
        lh = Esel[0:NB, b * 128:(b + 1) * 128]
        for i_ in range(2):
            A('pe', lambda e, i_=i_: e.matmul(pB[:, :], lhsT=lh, rhs=BChl[:, i_, 0:512], start=(i_ == 0), stop=(i_ == 1)),
              reads=[('Esel',), ('BChl', i_)], writes=[kB], inc=(i_ == 1))
        for i_ in range(2):
            A('pe', lambda e, i_=i_: e.matmul(pC[:, :], lhsT=lh, rhs=BChl[:, i_, 512:1024], start=(i_ == 0), stop=(i_ == 1)),
              reads=[('Esel',), ('BChl', i_)], writes=[kC], inc=(i_ == 1))
        for c in range(16):
            g = c // 4
            A('act', lambda e, c=c, g=g: e.activation(out=t1b[:, c, :], in_=pB[:, g * 128:(g + 1) * 128], func=AF.Copy, scale=xdt[:, c, b:b + 1]),
              reads=[kB, ('xdt',)], writes=[('t1b', c)])
        A('dve', lambda e: e.tensor_tensor(out=hb[:], in0=hb[:], in1=decP[:, :, b:b + 1].to_broadcast([128, 16, 128]), op=ALU.mult),
          reads=[('hbuf', bi, c_) for c_ in range(16)] + [('decP',)], writes=[('hbuf', bi, c_) for c_ in range(16)])
        A('dve', lambda e: e.tensor_tensor(out=hb[:].rearrange("p c n -> p (c n)"), in0=hb[:].rearrange("p c n -> p (c n)"),
                                           in1=t1b[:].rearrange("p c n -> p (c n)"), op=ALU.add),
          reads=[('hbuf', bi, c_) for c_ in range(16)] + [('t1b', c_) for c_ in range(16)], writes=[('hbuf', bi, c_) for c_ in range(16)])
        for c in range(16):
            g = c // 4
            A('dve', lambda e, c=c, g=g: e.scalar_tensor_tensor(out=junk[:, c % 4, :], in0=hb[:, c, :], scalar=1.0, in1=pC[:, g * 128:(g + 1) * 128], op0=ALU.mult, op1=ALU.mult,
                                                                accum_out=yS[:, c, b:b + 1]),
              reads=[('hbuf', bi, c), kC, ('yS', c)], writes=[('yS', c), ('junk', c % 4)])
        tr.dma('sp', nssm_s_d[b].rearrange("(c q) n -> q c n", q=128), hb[:], f'd_ho{bi}', reads=[('hbuf', bi, c_) for c_ in range(16)])

    while late_proj:
        proj_s(*late_proj.pop(0))
    wA = lambda tap: bc3(pf[:, P_CAW + tap * 8:P_CAW + tap * 8 + 8], 8)
    ci = t24[:, 0:8, :]
    co = c24[:, 0:8, :]
    A('dve', lambda e: e.tensor_tensor(out=ci, in0=projT[:, 8:16, :], in1=projT[:, 16:24, :], op=ALU.mult), reads=PJ, writes=[('t24',)])
    A('dve', lambda e: e.tensor_tensor(out=co, in0=stA[:, :, 0, :], in1=wA(0), op=ALU.mult), reads=[('stA',), ('pf',)], writes=[('c24',)])
    A('dve', lambda e: e.tensor_tensor(out=t8[:], in0=stA[:, :, 1, :], in1=wA(1), op=ALU.mult), reads=[('stA',), ('pf',)], writes=[('t8',)])
    A('dve', lambda e: e.tensor_tensor(out=co, in0=co, in1=t8[:], op=ALU.add), reads=[('c24',), ('t8',)], writes=[('c24',)])
    A('dve', lambda e: e.tensor_tensor(out=t8[:], in0=ci, in1=wA(2), op=ALU.mult), reads=[('t24',), ('pf',)], writes=[('t8',)])
    A('dve', lambda e: e.tensor_tensor(out=co, in0=co, in1=t8[:], op=ALU.add), reads=[('c24',), ('t8',)], writes=[('c24',)])
    A('dve', lambda e: e.tensor_tensor(out=ains[:], in0=projT[:, 0:8, :], in1=co, op=ALU.mult), reads=PJ + [('c24',)], writes=[('ains',)])
    A('act', lambda e: e.activation(out=nas[:, :, 0, :], in_=stA[:, :, 1, :], func=AF.Copy), reads=[('stA',)], writes=[('nas', 0)])
    A('act', lambda e: e.activation(out=nas[:, :, 1, :], in_=ci, func=AF.Copy), reads=[('t24',)], writes=[('nas', 1)])
    tr.dma('sp', nasT_d[:, :], nas[:].rearrange("p a b c -> p (a b c)"), 'd_nas', reads=[('nas', 0), ('nas', 1)])

    A('dve', lambda e: e.tensor_tensor(out=t16[:], in0=xcs[:, 0:16, :], in1=bc3(dskP[:], 16), op=ALU.mult), reads=[('xcs',), ('dskP',)], writes=[('t16',)])
    A('dve', lambda e: e.tensor_tensor(out=yS[:], in0=yS[:], in1=t16[:], op=ALU.add), reads=[('yS', c_) for c_ in range(16)] + [('t16',)], writes=[('yS', c_) for c_ in range(16)])
    A('dve', lambda e: e.tensor_tensor(out=yS[:], in0=yS[:], in1=zss[:], op=ALU.mult), reads=[('yS', c_) for c_ in range(16)] + [('zss',)], writes=[('yS', c_) for c_ in range(16)])
    A('dve', lambda e: e.tensor_tensor(out=sq[:], in0=yS[:], in1=yS[:], op=ALU.mult), reads=[('yS', c_) for c_ in range(16)], writes=[('sq',)])
    ps, pk = next_ps()
    for g in range(4):
        for i_ in range(4):
            A('pe', lambda e, g=g, i_=i_: e.matmul(ps[:, g * NB:(g + 1) * NB], lhsT=ones_f[:], rhs=sq[:, g * 4 + i_, :], start=(i_ == 0), stop=(i_ == 3)),
              reads=[('ones_f',), ('sq',)], writes=[pk], inc=(g == 3 and i_ == 3))
    rgv = rg[:].rearrange("p a b -> p (a b)")
    A('act', lambda e: e.activation(out=rgv, in_=ps[:, 0:4 * NB], func=AF.Ln, scale=1.0 / 512, bias=epsb[:, 0:1]), reads=[pk, ('epsb',)], writes=[('rg',)])
    A('act', lambda e: e.activation(out=rgv, in_=rgv, func=AF.Exp, scale=-0.5), reads=[('rg',)], writes=[('rg',)])
    for g in range(4):
        A('dve', lambda e, g=g: e.tensor_tensor(out=yS[:, g * 4:(g + 1) * 4, :], in0=yS[:, g * 4:(g + 1) * 4, :],
                                                in1=rg[:, g, :].unsqueeze(1).to_broadcast([128, 4, NB]), op=ALU.mult),
          reads=[('yS', c_) for c_ in range(16)] + [('rg',)], writes=[('yS', c_) for c_ in range(16)])
    A('dve', lambda e: e.tensor_tensor(out=ynTs[:], in0=yS[:], in1=bc3(pf[:, P_SNG:P_SNG + 16], 16), op=ALU.mult), reads=[('yS', c_) for c_ in range(16)] + [('pf',)], writes=[('ynTs',)])

    A('act', lambda e: e.activation(out=sga[:], in_=projT[:, 65:73, :], func=AF.Sigmoid), reads=PJ, writes=[('sga',)])
    A('act', lambda e: e.activation(out=sgb[:], in_=projT[:, 73:81, :], func=AF.Sigmoid), reads=PJ, writes=[('sgb',)])
    for q in range(2):
        slot, wk = wload(w_a[:, q * 512:(q + 1) * 512])
        ps, pk = next_ps()
        mm4(slot, wk, lambda k: ains[:, k, :], [('ains',)], 8, ps, pk)
        A('dve', lambda e, q=q: e.tensor_tensor(out=ma[:, 4 * q:4 * q + 4, :], in0=psv(ps), in1=sga[:, 4 * q:4 * q + 4, :], op=ALU.mult),
          reads=[pk, ('sga',)], writes=[('ma', q)])
    for q in range(2):
        s0 = wload(w_b[0:1024, q * 512:(q + 1) * 512])
        s1 = wload(w_b[1024:2048, q * 512:(q + 1) * 512])
        ps, pk = next_ps()
        mm4(None, None, lambda kk: ynTs[:, kk, :], [('ynTs',)], 16, ps, pk, lhs_slots=[s0, s1])
        A('dve', lambda e, q=q: e.tensor_tensor(out=r4[:], in0=psv(ps), in1=sgb[:, 4 * q:4 * q + 4, :], op=ALU.mult), reads=[pk, ('sgb',)], writes=[('r4',)])
        A('dve', lambda e, q=q: e.tensor_tensor(out=ma[:, 4 * q:4 * q + 4, :], in0=ma[:, 4 * q:4 * q + 4, :], in1=r4[:], op=ALU.add),
          reads=[('ma', q), ('r4',)], writes=[('ma', q)])
    A('dve', lambda e: e.tensor_copy(out=ms[:], in_=ma[:]), reads=[('ma', 0), ('ma', 1)], writes=[('ms',)])
    for q in range(2):
        slot, wk = wload(w_o[:, q * 512:(q + 1) * 512])
        ps, pk = next_ps()
        mm4(slot, wk, lambda k: ms[:, k, :], [('ms',)], 8, ps, pk)
        A('dve', lambda e, q=q: e.tensor_tensor(out=r4[:], in0=psv(ps), in1=modS[:, 16 + 4 * q:16 + 4 * q + 4, :], op=ALU.mult), reads=[pk] + MS, writes=[('r4',)])
        A('dve', lambda e, q=q: e.tensor_tensor(out=xs[:, 4 * q:4 * q + 4, :], in0=xs[:, 4 * q:4 * q + 4, :], in1=r4[:], op=ALU.add),
          reads=[('xs',), ('r4',)], writes=[('xs',)])

    mod_norm(A2s, ('A2s',), 24)
    for q in range(8):
        slot, wk = wload(w1[:, q * 512:(q + 1) * 512])
        ps, pk = next_ps()
        mm4(slot, wk, lambda k: uTs[:, k, :], [('uTs',)], 8, ps, pk)
        A('act', lambda e: e.activation(out=r4[:], in_=psv(ps), func=AF.Relu), reads=[pk], writes=[('r4',)])
        A('dve', lambda e, q=q: e.tensor_tensor(out=hms[:, 4 * q:4 * q + 4, :], in0=r4[:], in1=r4[:], op=ALU.mult), reads=[('r4',)], writes=[('hms', q)])
    for q in range(2):
        psA, kA = next_ps()
        for kg in range(4):
            slot, wk = wload(w2[kg * 1024:(kg + 1) * 1024, q * 512:(q + 1) * 512])
            for k in range(8):
                c = kg * 8 + k
                A('pe', lambda e, k=k, c=c: e.matmul(psA[0:NB, :], lhsT=hms[:, c, :], rhs=slot[:, k, :], start=(c == 0), stop=(c == 31)),
                  reads=list(wk) + [('hms', c // 4)], writes=[kA], inc=(k == 7))
        ps, pk = next_ps()
        tm_to_fm(psA, kA, ps, pk)
        A('dve', lambda e, q=q: e.tensor_tensor(out=r4[:], in0=psv(ps), in1=modS[:, 40 + 4 * q:40 + 4 * q + 4, :], op=ALU.mult), reads=[pk] + MS, writes=[('r4',)])
        A('dve', lambda e, q=q: e.tensor_tensor(out=xs[:, 4 * q:4 * q + 4, :], in0=xs[:, 4 * q:4 * q + 4, :], in1=r4[:], op=ALU.add),
          reads=[('xs',), ('r4',)], writes=[('xs',)])

    rms_s()
    A('dve', lambda e: e.tensor_tensor(out=t8[:], in0=xs[:], in1=rs[:].unsqueeze(1).to_broadcast([128, 8, NB]), op=ALU.mult),
      reads=[('xs',), ('rs',)], writes=[('t8',)])
    A('dve', lambda e: e.tensor_tensor(out=t8[:], in0=t8[:], in1=bc3(pf[:, P_NFG:P_NFG + 8], 8), op=ALU.mult), reads=[('t8',), ('pf',)], writes=[('t8',)])
    tr.dma('sp', ysT_d[:, :], v2(t8), 'd_ys', reads=[('t8',)])

    tr.finish('sp')
    es3.close()
    es.close()
    return nc


_NC = None


def _get_nc():
    global _NC
    if _NC is None:
        _NC = build()
    return _NC


def kernel(x_prompt, x_sample, c_prompt, c_sample, state_shortconv, state_ssm_conv, state_ssm,
           w_ada, b_ada, norm1_g, w_in, conv_a_w, w_a_out, conv_b_w, conv_b_b, dt_bias, a_log,
           d_skip, ssm_norm_g, w_b_out, w_o, norm2_g, w_mlp1, w_mlp2, norm_f_g):
    f = lambda a: np.ascontiguousarray(np.asarray(a, dtype=np.float32))
    fm = lambda v: np.asarray(v, np.float32).reshape(-1, 128).T
    pfm = np.concatenate([
        fm(norm1_g[0]),
        np.concatenate([fm(conv_a_w[0][t]) for t in range(3)], axis=1),
        np.concatenate([fm(conv_b_w[0][t]) for t in range(4)], axis=1),
        fm(conv_b_b[0]), fm(ssm_norm_g[0]), fm(norm2_g[0]), fm(b_ada[0]), fm(norm_f_g),
        np.concatenate([np.stack([np.asarray(v[0], np.float32) for v in (dt_bias, a_log, d_skip)], axis=1), np.zeros((96, 3), np.float32)], axis=0),
    ], axis=1)
    assert pfm.shape == (128, NPF)
    b_ada0 = np.asarray(b_ada[0], np.float32)
    prow = np.concatenate([np.asarray(dt_bias[0], np.float32), np.asarray(a_log[0], np.float32), np.asarray(d_skip[0], np.float32),
                           np.asarray(norm_f_g, np.float32)])[None, :]
    bg12 = np.stack([b_ada0[2048:3072], b_ada0[5120:6144]])
    assert prow.shape == (1, NPR)
    shared = {"pfm": f(pfm), "prow": f(prow), "bg12": f(bg12), "w_ada": f(w_ada[0]), "w_in": f(w_in[0]), "w_a_out": f(w_a_out[0]),
              "w_b_out": f(w_b_out[0]), "w_o": f(w_o[0]), "w_mlp1": f(w_mlp1[0]), "w_mlp2": f(w_mlp2[0])}
    in_maps = []
    for c in range(NCORES):
        m = dict(shared)
        m["xp"] = f(x_prompt[c])
        m["cpT"] = f(fm(c_prompt[c]))
        rs_ = slice(c * NSB, (c + 1) * NSB)
        m["xsT"] = f(np.asarray(x_sample[rs_, 0, :], np.float32).reshape(NSB, 8, 128).transpose(2, 1, 0).reshape(128, -1))
        m["csT"] = f(np.asarray(c_sample[rs_], np.float32).reshape(NSB, 8, 128).transpose(2, 1, 0).reshape(128, -1))
        m["stAT"] = f(np.asarray(state_shortconv[0, rs_], np.float32).reshape(NSB, 2, 8, 128).transpose(3, 2, 1, 0).reshape(128, -1))
        m["stBT"] = f(np.asarray(state_ssm_conv[0, rs_], np.float32).reshape(NSB, 3, 24, 128).transpose(3, 2, 1, 0).reshape(128, -1))
        m["sssm"] = f(np.asarray(state_ssm[0, rs_], np.float32).reshape(NSB, 2048, 128))
        in_maps.append(m)
    nc = _get_nc()
    res = run_bass_kernel_spmd(nc, in_maps, core_ids=list(range(NCORES)))
    R = res.results
    y_prompt = np.stack([R[c]["yp"] for c in range(NCORES)]).astype(np.float32)
    na_p = np.stack([R[c]["na_p"] for c in range(NCORES)])[None].astype(np.float32)
    nbc_p = np.stack([R[c]["nbc_p"] for c in range(NCORES)])[None].astype(np.float32)
    nssm_p = np.stack([R[c]["nssm_p"].reshape(32, 64, 128) for c in range(NCORES)])[None].astype(np.float32)
    y_sample = np.concatenate([R[c]["ysT"].reshape(128, 8, NSB).transpose(2, 1, 0).reshape(NSB, 1, D) for c in range(NCORES)]).astype(np.float32)
    na_s = np.concatenate([R[c]["nasT"].reshape(128, 8, 2, NSB).transpose(3, 2, 1, 0).reshape(NSB, 2, D) for c in range(NCORES)])[None].astype(np.float32)
    nbc_s = np.concatenate([R[c]["nbsT"].reshape(128, 24, 3, NSB).transpose(3, 2, 1, 0).reshape(NSB, 3, 3072) for c in range(NCORES)])[None].astype(np.float32)
    nssm_s = np.concatenate([R[c]["nssm_s"].reshape(NSB, 32, 64, 128) for c in range(NCORES)])[None].astype(np.float32)
    return (y_prompt, y_sample, na_p, nbc_p, nssm_p, na_s, nbc_s, nssm_s)
```

```python
import numpy as np
from contextlib import ExitStack
import concourse.bass as bass
import concourse.mybir as mybir
from concourse.bass_utils import run_bass_kernel_spmd

F32 = mybir.dt.float32
BF16 = mybir.dt.bfloat16
AF = mybir.ActivationFunctionType
ALU = mybir.AluOpType
AX = mybir.AxisListType

NCORES = 8
D = 1024
T = 512
import os
NTILE = int(os.environ.get('K_NTILE', '4'))
SEMMAX = int(os.environ.get('K_SEMMAX', '30000'))
PHSTOP = int(os.environ.get('K_PHSTOP', '99'))
DIN = 10272
O_BG, O_CG, O_HV, O_Z, O_XBC, O_DT, O_GA, O_GB = 0, 1024, 2048, 3072, 5120, 8192, 8224, 9248
EPS = 1e-6
P_N1G, P_CAW, P_CBW, P_CBB, P_SNG, P_N2G, P_BADA, P_NFG, P_HP, NPF = 0, 8, 32, 128, 152, 168, 176, 224, 232, 235
R_DTB, R_ALOG, R_DSK, R_NFG, NPR = 0, 32, 64, 96, 1120
NSB = 16


class Tr:
    def __init__(self, nc, es):
        self.nc, self.es = nc, es
        self.eng = {'pe': nc.tensor, 'act': nc.scalar, 'dve': nc.vector, 'pool': nc.gpsimd, 'sp': nc.sync}
        self.cnt = {e: 0 for e in self.eng}
        self.semi = {e: 0 for e in self.eng}
        self.sem = {}
        self.semname = {}
        for e in self.eng:
            self._newsem(e)
        self.seen = {e: {} for e in self.eng}
        self.lastw = {}
        self.reads = {}
        self.base = {}
        self.dsem = {}
        self.nwait = 0
        self.alias = {}

    def _canon(self, keys):
        return [self.alias.get(k, k) for k in keys]

    @staticmethod
    def _psum_excl(reads, writes):
        pr_ = [k for k in reads if k[0] in ('psf', 'psb')]
        if pr_:
            reads = [k for k in reads if k[0] not in ('psf', 'psb')]
            writes = list(writes) + pr_
        return reads, writes

    def _newsem(self, e):
        n = f"c_{e}_{self.semi[e]}"
        self.semi[e] += 1
        self.sem[e] = self.es.enter_context(self.nc.semaphore(n))
        self.semname[e] = n
        self.cnt[e] = 0

    def _need(self, e, dep, force, psum=False):
        name, sem, val, src = dep
        if src == e and not force and (e == 'pe' or psum):
            return
        if self.seen[e].get(name, 0) >= val:
            return
        self.eng[e].wait_ge(sem, val)
        self.nwait += 1
        self.seen[e][name] = val

    def _deps(self, e, reads, writes, isdma):
        for k in list(reads) + list(writes):
            b = self.base.get(k[0])
            if b:
                for d in b.values():
                    self._need(e, d, True)
        for k in reads:
            if k in self.lastw:
                self._need(e, self.lastw[k], True)
        for k in writes:
            ps_ = k[0] in ('psf', 'psb')
            if k in self.lastw:
                self._need(e, self.lastw[k], isdma, ps_)
            for d in self.reads.get(k, {}).values():
                self._need(e, d, isdma, ps_)

    def _commit(self, dep, reads, writes):
        for k in reads:
            self.reads.setdefault(k, {})[dep[0]] = dep
        for k in writes:
            self.lastw[k] = dep
            self.reads[k] = {}

    def op(self, e, fn, reads=(), writes=(), inc=True):
        reads, writes = self._canon(reads), self._canon(writes)
        reads, writes = self._psum_excl(reads, writes)
        if inc and self.cnt[e] >= SEMMAX:
            self._newsem(e)
        self._deps(e, reads, writes, False)
        ins = fn(self.eng[e])
        if inc:
            self.cnt[e] += 1
            ins.then_inc(self.sem[e], 1)
            val = self.cnt[e]
        else:
            val = self.cnt[e] + 1
        self._commit((self.semname[e], self.sem[e], val, e), reads, writes)

    def dma(self, q, out, in_, semname, reads=(), writes=(), **kw):
        reads, writes = self._canon(reads), self._canon(writes)
        self._deps(q, reads, writes, True)
        if semname not in self.dsem:
            self.dsem[semname] = [self.es.enter_context(self.nc.semaphore(semname)), 0]
        s = self.dsem[semname]
        s[1] += 16
        self.eng[q].dma_start(out=out, in_=in_, **kw).then_inc(s[0], 16)
        self._commit((semname, s[0], s[1], 'dma'), reads, writes)

    def dma_multi(self, q, items, semname):
        for (out, in_, writes) in items:
            self._deps(q, (), writes, True)
        if semname not in self.dsem:
            self.dsem[semname] = [self.es.enter_context(self.nc.semaphore(semname)), 0]
        s = self.dsem[semname]
        for (out, in_, writes) in items:
            s[1] += 16
            self.eng[q].dma_start(out=out, in_=in_).then_inc(s[0], 16)
        for (out, in_, writes) in items:
            self._commit((semname, s[0], s[1], 'dma'), (), writes)

    def retire(self, buf):
        b = self.base.setdefault(buf, {})
        for k in list(self.lastw):
            if k[0] == buf:
                d = self.lastw.pop(k)
                if d[0] not in b or b[d[0]][2] < d[2]:
                    b[d[0]] = d
        for k in list(self.reads):
            if k[0] == buf:
                for d in self.reads.pop(k).values():
                    if d[0] not in b or b[d[0]][2] < d[2]:
                        b[d[0]] = d

    def barrier(self):
        for e in self.eng:
            for f in self.eng:
                if f != e and self.cnt[f] > 0:
                    self._need(e, (self.semname[f], self.sem[f], self.cnt[f], f), True)
            for name, (sem, val) in self.dsem.items():
                self._need(e, (name, sem, val, 'dma'), True)
        self.lastw.clear()
        self.reads.clear()
        self.base.clear()

    def finish(self, e='sp'):
        for name, (sem, val) in self.dsem.items():
            if self.seen[e].get(name, 0) < val:
                self.eng[e].wait_ge(sem, val)


def build():
    nc = bass.Bass("TRN2", target_bir_lowering=False)
    es = ExitStack()
    tr = Tr(nc, es)

    def din(n, s):
        return nc.dram_tensor(n, list(s), F32, kind="ExternalInput").ap()

    def dout(n, s):
        return nc.dram_tensor(n, list(s), F32, kind="ExternalOutput").ap()

    def sb(n, s, dt=F32):
        return es.enter_context(nc.sbuf_tensor(n, list(s), dt))

    A = tr.op
    xp = din("xp", [2048, D])
    cpT = din("cpT", [128, 8])
    pfm = din("pfm", [128, NPF])
    prow = din("prow", [1, NPR])
    bg12 = din("bg12", [2, D])
    w_ada = din("w_ada", [D, 6 * D])
    w_in = din("w_in", [D, DIN])
    w_a = din("w_a_out", [D, D])
    w_b = din("w_b_out", [2 * D, D])
    w_o = din("w_o", [D, D])
    w1 = din("w_mlp1", [D, 4 * D])
    w2 = din("w_mlp2", [4 * D, D])
    yp = dout("yp", [2048, D])
    na_p = dout("na_p", [2, D])
    nbc_p = dout("nbc_p", [3, 3072])
    nssm_p = dout("nssm_p", [2048, 128])

    ident_b = sb("ident_b", [128, 128], BF16)
    ident_f = sb("ident_f", [128, 128])
    U = sb("U", [128, 128])
    ones_f = sb("ones_f", [128, 128])
    Esel = sb("Esel", [32, 32 * 128], BF16)
    A('pool', lambda e: e.memset(ident_b[:], 0.0), writes=[('ident_b',)])
    A('pool', lambda e: e.affine_select(out=ident_b[:], in_=ident_b[:], pattern=[[-1, 128]], compare_op=ALU.not_equal,
                                        fill=1.0, base=0, channel_multiplier=1), reads=[('ident_b',)], writes=[('ident_b',)])
    A('pool', lambda e: e.memset(ident_f[:], 0.0), writes=[('ident_f',)])
    A('pool', lambda e: e.affine_select(out=ident_f[:], in_=ident_f[:], pattern=[[-1, 128]], compare_op=ALU.not_equal,
                                        fill=1.0, base=0, channel_multiplier=1), reads=[('ident_f',)], writes=[('ident_f',)])
    A('pool', lambda e: e.memset(U[:], 1.0), writes=[('U',)])
    A('pool', lambda e: e.affine_select(out=U[:], in_=U[:], pattern=[[1, 128]], compare_op=ALU.is_ge,
                                        fill=0.0, base=0, channel_multiplier=-1), reads=[('U',)], writes=[('U',)])
    A('pool', lambda e: e.memset(ones_f[:], 1.0), writes=[('ones_f',)])
    NEGM = sb("NEGM", [128, 512], BF16)
    A('pool', lambda e: e.memset(NEGM[:], -30000.0), writes=[('NEGM',)])
    nv = NEGM[:].rearrange("p (a b) -> p a b", b=128)
    A('pool', lambda e: e.affine_select(out=nv, in_=nv, pattern=[[0, 4], [-1, 128]], compare_op=ALU.is_gt,
                                        fill=0.0, base=0, channel_multiplier=1), reads=[('NEGM',)], writes=[('NEGM',)])
    A('pool', lambda e: e.memset(Esel[:], 0.0), writes=[('Esel',)])
    ev = Esel[:].rearrange("p (h s) -> p h s", s=128)
    A('pool', lambda e: e.affine_select(out=ev, in_=ev, pattern=[[1, 32], [0, 128]], compare_op=ALU.not_equal,
                                        fill=1.0, base=0, channel_multiplier=-1), reads=[('Esel',)], writes=[('Esel',)])

    NPSF = 6
    psf = [es.enter_context(nc.psum_tensor(f"psf{i}", [128, 512], F32)) for i in range(NPSF)]
    psb = [es.enter_context(nc.psum_tensor(f"psb{i}", [128, 1024], BF16)) for i in range(2)]
    st = {'ps': 0, 'pt': 0, 'ws': 0}

    def next_ps():
        i = st['ps'] % NPSF
        st['ps'] += 1
        return psf[i], ('psf', i)

    def next_pt():
        i = st['pt'] % 2
        st['pt'] += 1
        return psb[i][:, 0:512], ('psb', i)

    NSLOT = 3
    slots = [sb(f"ws{i}", [128, 8, 512], BF16) for i in range(NSLOT)]
    wsl = {'slots': slots, 'n': NSLOT, 'base': 0}

    class WK(tuple):
        pass

    NSCR = 40
    wscr = nc.dram_tensor("wscr", [NSCR, 128, 8 * 512], BF16).ap()
    wmap = {}
    wdump = {'on': False, 'use': False}

    def wkey(src):
        return (src.name, str(src.offset), tuple(src.shape))

    def wload(src):
        i = wsl['base'] + st['ws'] % wsl['n']
        st['ws'] += 1
        keys = [('ws', i, n_) for n_ in range(4)]
        sl_ = wsl['slots'][i - wsl['base']]
        k_ = wkey(src)
        if wdump['use'] and k_ in wmap:
            idx = wmap[k_]
            tr.dma('pool', sl_[:, :, :], wscr[idx].rearrange("p (k c) -> p k c", c=512), f"d_ws{i}", reads=[('wscr', idx)], writes=keys)
            return sl_, keys
        tr.dma('pool', sl_[:, :, :], src.rearrange("(k p) c -> p k c", p=128), f"d_ws{i}", writes=keys)
        if wdump['on'] and len(wmap) < NSCR and k_ not in wmap:
            idx = len(wmap)
            wmap[k_] = idx
            tr.dma('sp', wscr[idx].rearrange("p (k c) -> p k c", c=512), sl_[:, :, :], f"d_wd{i}", reads=keys, writes=[('wscr', idx)])
        return sl_, keys

    def wload3(srcs):
        i = st['ws'] % NSLOT
        st['ws'] += 1
        items = []
        for n_, src in enumerate(srcs):
            items.append((slots[i][:, :, n_ * 128:(n_ + 1) * 128], src.rearrange("(k p) c -> p k c", p=128), [('ws', i, n_)]))
        tr.dma_multi('pool', items, f"d_ws{i}")
        return slots[i], [('ws', i, n_) for n_ in range(4)]

    cp = sb("cp", [128, 8])
    sc = sb("sc", [128, 8])
    sc_rep = sb("sc_rep", [128, 8, 128], BF16)
    pf = sb("pf", [128, NPF])
    pr = sb("pr", [128, NPR])
    wdt = sb("wdt", [128, 8, 32], BF16)
    modT = sb("modT", [128, 48])
    A1 = sb("A1", [128, 8])
    A2 = sb("A2", [128, 8])
    g1bc = sb("g1bc", [128, D])
    g2bc = sb("g2bc", [128, D])
    scr = sb("scr", [128, 6, 515])
    cfull = scr[:, 0:2, :]
    cacc = scr[:, 2:4, 0:512]
    hvs = scr[:, 4:6, 0:512]
    ysb = scr[:, 0:2, 0:512]
    ytmp = scr[:, 2:4, 0:512]
    relu = scr[:, 4:6, 0:512]
    mtmp = scr[:, 0:2, 0:512]
    mtmp2 = scr[:, 2, 0:512]
    sto = scr[:, 0:2, 0:128]
    for i_ in range(2):
        tr.alias[('cfull', i_)] = ('scr', i_)
        tr.alias[('cfullp', i_)] = ('scrp', i_)
        tr.alias[('cacc', i_)] = ('scr', 2 + i_)
    for i_ in range(2):
        tr.alias[('hvs', i_)] = ('scr', 4 + i_)
        tr.alias[('ysb', i_)] = ('scr', i_)
        tr.alias[('ytmp', i_)] = ('scr', 2 + i_)
        tr.alias[('relu', i_)] = ('scr', 4 + i_)
        tr.alias[('mtmp', i_)] = ('scr', i_)
        tr.alias[('sto', i_)] = ('scr', i_)
    tr.alias[('mtmp2',)] = ('scr', 2)
    for i_ in range(4):
        tr.alias[('xnb', i_)] = ('X', 'n', i_)
        tr.alias[('xtm', 0, i_)] = ('X', 't', i_)
        tr.alias[('ynb', i_)] = ('X', 'y', i_)
    for i_ in range(8):
        tr.alias[('sgbT', i_)] = ('ainT', i_)
    epsb = sb("epsb", [128, 1])
    A('dve', lambda e: e.memset(epsb[:], EPS), writes=[('epsb',)])
    eabc = sb("eabc", [128, 32])
    tr.dma('sp', cp[:], cpT[:, :], 'd_cp', writes=[('cp',)])
    tr.dma('sp', pf[:], pfm[:, :], 'd_pf', writes=[('pf',)])
    tr.dma('sp', pr[:], prow[0:1, :].partition_broadcast(128), 'd_pr', writes=[('pr',)])
    tr.dma('sp', g1bc[:], bg12[0:1, :].partition_broadcast(128), 'd_g1', writes=[('g1bc',)])
    tr.dma('sp', g2bc[:], bg12[1:2, :].partition_broadcast(128), 'd_g2', writes=[('g2bc',)])
    tr.dma('pool', wdt[:], w_in[:, O_DT:O_DT + 32].rearrange("(k p) c -> p k c", p=128), 'd_wdt', writes=[('wdt',)])
    A('act', lambda e: e.activation(out=sc[:], in_=cp[:], func=AF.Silu), reads=[('cp',)], writes=[('sc',)])
    A('dve', lambda e: e.tensor_copy(out=sc_rep[:], in_=sc[:].unsqueeze(2).to_broadcast([128, 8, 128])),
      reads=[('sc',)], writes=[('sc_rep',)])
    A('act', lambda e: e.activation(out=eabc[:], in_=pr[:, R_ALOG:R_ALOG + 32], func=AF.Exp), reads=[('pr',)], writes=[('eabc',)])

    NB = NSB
    csT_d = din("csT", [128, 8 * NB])
    cs = sb("cs", [128, 8, NB])
    scs = sb("scs", [128, 8, NB], BF16)
    modS = sb("modS", [128, 48, NB])
    v2 = lambda t: t[:].rearrange("p a b -> p (a b)")
    tr.dma('sp', v2(cs), csT_d[:, :], 'd_cs', writes=[('cs',)])
    A('act', lambda e: e.activation(out=v2(scs), in_=v2(cs), func=AF.Silu), reads=[('cs',)], writes=[('scs',)])

    def bc3(ap2, n_):
        return ap2.unsqueeze(2).to_broadcast([128, n_, NB])

    tmT = scr[0:NB, 4:6, 0:512]
    st_tm = {'i': 0}

    def tm_to_fm(psA, kA, ps, pk):
        ti_ = st_tm['i'] % 2
        st_tm['i'] += 1
        A('act', lambda e: e.activation(out=tmT[:, ti_, :], in_=psA[0:NB, :], func=AF.Copy), reads=[kA], writes=[('scr', 4 + ti_)])
        for jj in range(4):
            A('pe', lambda e, jj=jj: e.transpose(ps[:, jj * NB:(jj + 1) * NB], tmT[:, ti_, jj * 128:(jj + 1) * 128], ident_f[0:NB, 0:NB]),
              reads=[('scr', 4 + ti_), ('ident_f',)], writes=[pk], inc=(jj == 3))

    def mm4(slot, wk, rhs_fn, rkeys, nk, ps, pk, lhs_slots=None):
        psA, kA = next_ps()
        for k in range(nk):
            sl_, wk_ = (slot, wk) if lhs_slots is None else lhs_slots[k // 8]
            A('pe', lambda e, k=k, sl_=sl_: e.matmul(psA[0:NB, :], lhsT=rhs_fn(k), rhs=sl_[:, k % 8, :], start=(k == 0), stop=(k == nk - 1)),
              reads=list(wk_) + rkeys, writes=[kA], inc=(k == nk - 1))
        tm_to_fm(psA, kA, ps, pk)

    psv = lambda ps: ps[:, 0:4 * NB].rearrange("p (a b) -> p a b", b=NB)

    for cg in range(12):
        slot, wk = wload(w_ada[:, cg * 512:(cg + 1) * 512])
        ps, pk = next_ps()
        for k in range(8):
            A('pe', lambda e, k=k: e.matmul(ps[:, :], lhsT=sc_rep[:, k, :], rhs=slot[:, k, :], start=(k == 0), stop=(k == 7)),
              reads=[('sc_rep',)] + wk, writes=[pk], inc=(k == 7))
        mi = cg % 2
        A('act', lambda e: e.activation(out=mtmp[:, mi, :], in_=ps[:, :], func=AF.Copy), reads=[pk], writes=[('mtmp', mi)])
        A('dve', lambda e: e.tensor_tensor(out=mtmp2[:].rearrange("p (a b) -> p a b", b=128),
                                           in0=mtmp[:, mi, :].rearrange("p (a b) -> p a b", b=128),
                                           in1=ident_f[:].unsqueeze(1).to_broadcast([128, 4, 128]), op=ALU.mult),
          reads=[('mtmp', mi), ('ident_f',)], writes=[('mtmp2',)])
        A('dve', lambda e: e.tensor_reduce(out=modT[:, 4 * cg:4 * cg + 4], in_=mtmp2[:].rearrange("p (a b) -> p a b", b=128),
                                           axis=AX.X, op=ALU.add), reads=[('mtmp2',)], writes=[('modT',)])
        if cg in (4, 5):
            o = (cg - 4) * 512
            A('dve', lambda e: e.tensor_tensor(out=g1bc[:, o:o + 512], in0=mtmp[:, mi, :], in1=g1bc[:, o:o + 512], op=ALU.add),
              reads=[('mtmp', mi), ('g1bc',)], writes=[('g1bc',)])
        if cg in (10, 11):
            o = (cg - 10) * 512
            A('dve', lambda e: e.tensor_tensor(out=g2bc[:, o:o + 512], in0=mtmp[:, mi, :], in1=g2bc[:, o:o + 512], op=ALU.add),
              reads=[('mtmp', mi), ('g2bc',)], writes=[('g2bc',)])
        ps_s, pk_s = next_ps()
        mm4(slot, wk, lambda k: scs[:, k, :], [('scs',)], 8, ps_s, pk_s)
        A('dve', lambda e: e.tensor_tensor(out=modS[:, 4 * cg:4 * cg + 4, :], in0=psv(ps_s), in1=bc3(pf[:, P_BADA + 4 * cg:P_BADA + 4 * cg + 4], 4), op=ALU.add),
          reads=[pk_s, ('pf',)], writes=[('modS', cg)])
    A('dve', lambda e: e.tensor_tensor(out=modT[:], in0=modT[:], in1=pf[:, P_BADA:P_BADA + 48], op=ALU.add),
      reads=[('modT',), ('pf',)], writes=[('modT',)])
    A('dve', lambda e: e.scalar_tensor_tensor(out=A1[:], in0=modT[:, 8:16], scalar=1.0, in1=pf[:, P_N1G:P_N1G + 8], op0=ALU.add, op1=ALU.mult),
      reads=[('modT',), ('pf',)], writes=[('A1',)])
    A('dve', lambda e: e.scalar_tensor_tensor(out=A2[:], in0=modT[:, 32:40], scalar=1.0, in1=pf[:, P_N2G:P_N2G + 8], op0=ALU.add, op1=ALU.mult),
      reads=[('modT',), ('pf',)], writes=[('A2',)])

    es2 = ExitStack()

    def sb2(n, s_, dt=F32):
        return es2.enter_context(nc.sbuf_tensor(n, list(s_), dt))

    bufs = [sb2("bufA", [128, 4, D]), sb2("bufB", [128, 4, D])]
    ss_n = sb2("ss_n", [128, 8])
    rstd_n = sb2("rstd_n", [128, 8])
    X = sb2("X", [128, 4096], BF16)
    xnb = X[:, :].rearrange("p (s f) -> p s f", f=D)
    xtm = X[:, 0:2048].rearrange("p (o f) -> p o f", o=1)
    ynb = X[:, 2048:4096]
    uT = sb2("uT", [128, 8, T], BF16)
    ainT = sb2("ainT", [128, 8, T], BF16)
    sgT = sb2("sgT", [128, 8, T], BF16)
    sgbT = ainT
    R1 = sb2("R1", [128, 16384], BF16)
    xcT = R1[:, 0:12288].rearrange("p (c t) -> p c t", t=T)
    Btm = R1[:, 12288:12800]
    xpp = R1[:, 14336:16384]
    hmT = R1[:, :].rearrange("p (c t) -> p c t", t=T)
    ynT = sb2("ynT", [128, 16, T], BF16)
    hT = sb2("hT", [128, 2048])
    hTb = sb2("hTb", [128, 2048], BF16)
    prevA = sb2("prevA", [128, 2, 8])
    prevB = sb2("prevB", [128, 3, 24])
    WT4 = sb2("WT4", [128, 2, 512], BF16)
    CBs = sb2("CBs", [128, 512])
    e4 = sb2("e4", [128, 2, 512], BF16)
    ddt4 = sb2("ddt4", [128, 4, 32])
    ss = sb2("ss", [128, 8])
    rstd = sb2("rstd", [128, 8])
    dtt = sb2("dtt", [128, 4, 32])
    dta = sb2("dta", [128, 4, 32])
    dtmp = sb2("dtmp", [128, 4, 32])
    dtmp2 = sb2("dtmp2", [128, 4, 32])
    acs4 = sb2("acs4", [128, 4, 32])
    nacs4 = sb2("nacs4", [128, 4, 32])
    eacs4 = sb2("eacs4", [128, 4, 32])
    d24 = sb2("d24", [128, 4, 32])
    ealb4 = sb2("ealb4", [128, 4, 32])
    acsT4 = sb2("acsT4", [32, 4, 2, 128], BF16)
    acsTf = sb2("acsTf", [32, 128])
    gss = sb2("gss", [128, 8])
    grs = sb2("grs", [128, 8])

    print('SBUF remaining after per-tile alloc (KB/partition):', nc.sbuf_bytes_remaining / 1024.0)
    A('dve', lambda e: e.memset(hT[:], 0.0), writes=[('hT', g) for g in range(4)])
    A('dve', lambda e: e.memset(hTb[:], 0.0), writes=[('hTb', g) for g in range(4)])
    A('dve', lambda e: e.memset(prevA[:], 0.0), writes=[('prevA', j) for j in range(8)])
    A('dve', lambda e: e.memset(prevB[:], 0.0), writes=[('prevB', c) for c in range(24)])

    def norm_to_T(A_sc, B_sc, Akey, Bkey, xb=None, xk=None, pre=False):
        xb = xt if xb is None else xb
        xk = (lambda s: ('xt', s)) if xk is None else xk
        rs_, rk_ = (rstd_n, ('rstd_n',)) if pre else (rstd, ('rstd',))
        if not pre:
            A('dve', lambda e: e.memset(ss[:, 0:4], 0.0), writes=[('ss',)])
            for s in range(4):
                A('act', lambda e, s=s: e.activation(out=xnb[:, s, :], in_=xb[:, s, :], func=AF.Square, accum_out=ss[:, s:s + 1]),
                  reads=[xk(s), ('ss',)], writes=[('xnb', s), ('ss',)])
            A('act', lambda e: e.activation(out=rstd[:, 4:8], in_=ss[:, 0:4], func=AF.Ln, scale=1.0 / D, bias=epsb[:, 0:1]),
              reads=[('ss',), ('epsb',)], writes=[('rstd_t',)])
            A('act', lambda e: e.activation(out=rstd[:, 0:4], in_=rstd[:, 4:8], func=AF.Exp, scale=-0.5), reads=[('rstd_t',)], writes=[('rstd',)])
        for s in range(4):
            A('dve', lambda e, s=s: e.tensor_scalar(out=xnb[:, s, :], in0=xb[:, s, :], scalar1=rs_[:, s:s + 1], scalar2=None, op0=ALU.mult),
              reads=[xk(s), rk_], writes=[('xnb', s)])
        for k in range(8):
            pt, ptk = next_pt()
            for s in range(4):
                A('pe', lambda e, s=s, k=k: e.transpose(pt[:, s * 128:(s + 1) * 128], xnb[:, s, k * 128:(k + 1) * 128], ident_b[:]),
                  reads=[('xnb', s), ('ident_b',)], writes=[ptk], inc=(s == 3))
            A('act', lambda e, k=k: e.activation(out=uT[:, k, :], in_=pt, func=AF.Identity, scale=A_sc[:, k:k + 1], bias=B_sc[:, k:k + 1]),
              reads=[ptk, Akey, Bkey], writes=[('uT', k)])

    def proj_fm(src_cols, evac):
        slot, wk = wload(src_cols)
        for jj in range(4):
            ps, pk = next_ps()
            for k in range(8):
                A('pe', lambda e, k=k, jj=jj: e.matmul(ps[:, :], lhsT=slot[:, k, jj * 128:(jj + 1) * 128], rhs=uT[:, k, :],
                                                         start=(k == 0), stop=(k == 7)),
                  reads=[wk[jj], ('uT', k)], writes=[pk], inc=(k == 7))
            evac(jj, ps, pk)

    for ti in range(NTILE):
        last = (ti == NTILE - 1)
        wdump['on'] = last
        t0 = ti * T
        P_, Q_ = ti % 2, (ti + 1) % 2
        Pn, Qn = f'B{P_}', f'B{Q_}'
        xt = bufs[P_]
        zs = bufs[Q_][:].bitcast(BF16)
        for s in range(4):
            tr.alias[('xt', s)] = (Pn, 'x', s)
            for cg in range(4):
                tr.alias[('zs', s, cg)] = (Qn, 'z', s, cg)
        tr.retire('R1')
        tr.retire('X')
        if ti == 0:
            for s in range(4):
                tr.dma('sp', xt[:, s, :], xp[t0 + s * 128:t0 + (s + 1) * 128, :], f'd_xt{s}', writes=[('xt', s)])
            norm_to_T(A1, modT[:, 0:8], ('A1',), ('modT',))
        else:
            tr.retire(Qn)

        ps, pk = next_ps()
        for s in range(4):
            for k in range(8):
                A('pe', lambda e, k=k, s=s: e.matmul(ps[:, s * 32:(s + 1) * 32], lhsT=uT[:, k, s * 128:(s + 1) * 128], rhs=wdt[:, k, :],
                                                      start=(k == 0), stop=(k == 7)),
                  reads=[('wdt',), ('uT', k)], writes=[pk], inc=(k == 7 and s == 3))
        dtv = lambda t: t[:].rearrange("p s h -> p (s h)")
        A('dve', lambda e: e.tensor_tensor(out=dtmp[:], in0=ps[:, 0:128].rearrange("p (s h) -> p s h", h=32),
                                           in1=pr[:, R_DTB:R_DTB + 32].unsqueeze(1).to_broadcast([128, 4, 32]), op=ALU.add),
          reads=[pk, ('pr',)], writes=[('dtmp',)])
        A('dve', lambda e: e.scalar_tensor_tensor(out=dtv(dtmp2), in0=dtv(dtmp), scalar=-1.0, in1=dtv(dtmp), op0=ALU.mult, op1=ALU.max),
          reads=[('dtmp',)], writes=[('dtmp2',)])
        A('act', lambda e: e.activation(out=dtv(dtmp2), in_=dtv(dtmp2), func=AF.Exp, scale=-1.0), reads=[('dtmp2',)], writes=[('dtmp2',)])
        A('act', lambda e: e.activation(out=dtv(dtmp2), in_=dtv(dtmp2), func=AF.Ln, bias=1.0), reads=[('dtmp2',)], writes=[('dtmp2',)])
        A('dve', lambda e: e.scalar_tensor_tensor(out=dtv(dtt), in0=dtv(dtmp), scalar=0.0, in1=dtv(dtmp2), op0=ALU.max, op1=ALU.add),
          reads=[('dtmp',), ('dtmp2',)], writes=[('dtt',)])
        A('dve', lambda e: e.scalar_tensor_tensor(out=dta[:], in0=dtt[:], scalar=-1.0, in1=eabc[:].unsqueeze(1).to_broadcast([128, 4, 32]),
                                                  op0=ALU.mult, op1=ALU.mult),
          reads=[('dtt',), ('eabc',)], writes=[('dta',)])

        def decay_block():
            for s in range(4):
                ps, pk = next_ps()
                A('pe', lambda e: e.matmul(ps[:, 0:32], lhsT=U[:], rhs=dta[:, s, :], start=True, stop=True), reads=[('U',), ('dta',)], writes=[pk], inc=False)
                A('pe', lambda e: e.matmul(ps[:, 32:64], lhsT=ones_f[:], rhs=dta[:, s, :], start=True, stop=True), reads=[('ones_f',), ('dta',)], writes=[pk], inc=False)
                A('pe', lambda e: e.matmul(ps[0:32, 128:256], lhsT=dta[:, s, :], rhs=U[:], start=True, stop=True), reads=[('U',), ('dta',)], writes=[pk])
                A('act', lambda e: e.activation(out=acsTf[:], in_=ps[0:32, 128:256], func=AF.Copy), reads=[pk], writes=[('acsTf',)])
                A('act', lambda e: e.activation(out=acs4[:, s, :], in_=ps[:, 0:32], func=AF.Copy), reads=[pk], writes=[('acs4', s)])
                A('act', lambda e: e.activation(out=nacs4[:, s, :], in_=ps[:, 0:32], func=AF.Copy, scale=-1.0), reads=[pk], writes=[('nacs4', s)])
                A('act', lambda e: e.activation(out=eacs4[:, s, :], in_=ps[:, 0:32], func=AF.Exp), reads=[pk], writes=[('eacs4', s)])
                A('act', lambda e: e.activation(out=ealb4[:, s, :], in_=ps[:, 32:64], func=AF.Exp), reads=[pk], writes=[('ealb4', s)])
                A('dve', lambda e: e.tensor_tensor(out=d24[:, s, :], in0=ps[:, 32:64], in1=acs4[:, s, :], op=ALU.subtract), reads=[pk, ('acs4', s)], writes=[('d24', s)])
                A('act', lambda e: e.activation(out=d24[:, s, :], in_=d24[:, s, :], func=AF.Exp), reads=[('d24', s)], writes=[('d24', s)])
                A('dve', lambda e: e.tensor_copy(out=acsT4[:, s, 0, :], in_=acsTf[:]), reads=[('acsTf',)], writes=[('acsT4', s, 0)])
                A('dve', lambda e: e.tensor_tensor(out=acsTf[:], in0=acsTf[:], in1=acsT4[:, s, 0, :], op=ALU.subtract),
                  reads=[('acsTf',), ('acsT4', s, 0)], writes=[('acsTf',)])
                A('dve', lambda e: e.tensor_copy(out=acsT4[:, s, 1, :], in_=acsTf[:]), reads=[('acsTf',)], writes=[('acsT4', s, 1)])
            A('dve', lambda e: e.reciprocal(out=dtv(ddt4), in_=dtv(dtt)), reads=[('dtt',)], writes=[('ddt4',)])
            A('dve', lambda e: e.tensor_tensor(out=ddt4[:], in0=ddt4[:], in1=pr[:, R_DSK:R_DSK + 32].unsqueeze(1).to_broadcast([128, 4, 32]), op=ALU.mult),
              reads=[('ddt4',), ('pr',)], writes=[('ddt4',)])


        for j in range(8):
            if True:
                slot, wk = wload3([w_in[:, o + j * 128:o + (j + 1) * 128] for o in (O_BG, O_CG, O_HV)])
                pss = []
                for n_ in range(3):
                    ps, pk = next_ps()
                    for k in range(8):
                        A('pe', lambda e, k=k, ps=ps: e.matmul(ps[:, :], lhsT=slot[:, k, n_ * 128:(n_ + 1) * 128], rhs=uT[:, k, :],
                                                                start=(k == 0), stop=(k == 7)),
                          reads=[wk[n_], ('uT', k)], writes=[pk], inc=(k == 7))
                    pss.append((ps, pk))
                (pbg, kbg), (pcg, kcg), (phv, khv) = pss
                hi = j % 2
                ci = j % 2
                A('act', lambda e: e.activation(out=hvs[:, hi, :], in_=phv[:, :], func=AF.Copy), reads=[khv], writes=[('hvs', hi)])
                A('act', lambda e: e.activation(out=cfull[:, ci, 0:2], in_=prevA[:, :, j], func=AF.Copy), reads=[('prevA', j)], writes=[('cfullp', ci)])
                A('dve', lambda e: e.tensor_tensor(out=cfull[:, ci, 2:514], in0=pcg[:, :], in1=hvs[:, hi, :], op=ALU.mult),
                  reads=[kcg, ('hvs', hi)], writes=[('cfull', ci)])
                A('dve', lambda e: e.tensor_scalar(out=cacc[:, ci, :], in0=cfull[:, ci, 0:512], scalar1=pf[:, P_CAW + j:P_CAW + j + 1], scalar2=None, op0=ALU.mult),
                  reads=[('cfull', ci), ('cfullp', ci), ('pf',)], writes=[('cacc', ci)])
                for tap in (1, 2):
                    A('dve', lambda e, tap=tap: e.scalar_tensor_tensor(out=cacc[:, ci, :], in0=cfull[:, ci, tap:tap + 512],
                                                                       scalar=pf[:, P_CAW + tap * 8 + j:P_CAW + tap * 8 + j + 1],
                                                                       in1=cacc[:, ci, :], op0=ALU.mult, op1=ALU.add),
                      reads=[('cfull', ci), ('cfullp', ci), ('cacc', ci)], writes=[('cacc', ci)])
                A('dve', lambda e: e.tensor_tensor(out=ainT[:, j, :], in0=pbg[:, :], in1=cacc[:, ci, :], op=ALU.mult),
                  reads=[kbg, ('cacc', ci)], writes=[('ainT', j)])
                A('act', lambda e: e.activation(out=prevA[:, :, j], in_=cfull[:, ci, 512:514], func=AF.Copy),
                  reads=[('cfull', ci)], writes=[('prevA', j)])
                if j == 1:
                    decay_block()

        def z_group(cg):
            slot, wk = wload(w_in[:, O_Z + cg * 512:O_Z + (cg + 1) * 512])
            for s in range(4):
                ps, pk = next_ps()
                for k in range(8):
                    A('pe', lambda e, k=k, s=s: e.matmul(ps[:, :], lhsT=uT[:, k, s * 128:(s + 1) * 128], rhs=slot[:, k, :],
                                                          start=(k == 0), stop=(k == 7)),
                      reads=wk + [('uT', k)], writes=[pk], inc=(k == 7))
                A('act', lambda e, s=s: e.activation(out=zs[:, s, cg * 512:(cg + 1) * 512], in_=ps[:, :], func=AF.Silu),
                  reads=[pk], writes=[('zs', s, cg)])

        xbc_tail = {'f': None}
        for q in range(6):
            def evac_xbc(jj, ps, pk, q=q):
                c = q * 4 + jj
                ci = c % 2
                A('act', lambda e: e.activation(out=cfull[:, ci, 3:515], in_=ps[:, :], func=AF.Copy), reads=[pk], writes=[('cfull', ci)])
                A('act', lambda e: e.activation(out=cfull[:, ci, 0:3], in_=prevB[:, :, c], func=AF.Copy), reads=[('prevB', c)], writes=[('cfullp', ci)])
                A('act', lambda e: e.activation(out=cacc[:, ci, :], in_=ps[:, :], func=AF.Identity, scale=pf[:, P_CBW + 72 + c:P_CBW + 72 + c + 1],
                                                bias=pf[:, P_CBB + c:P_CBB + c + 1]),
                  reads=[pk, ('pf',)], writes=[('cacc', ci)])
                for tap in (0, 1, 2):
                    A('dve', lambda e, tap=tap: e.scalar_tensor_tensor(out=cacc[:, ci, :], in0=cfull[:, ci, tap:tap + 512],
                                                                       scalar=pf[:, P_CBW + tap * 24 + c:P_CBW + tap * 24 + c + 1],
                                                                       in1=cacc[:, ci, :], op0=ALU.mult, op1=ALU.add),
                      reads=[('cfull', ci), ('cfullp', ci), ('cacc', ci)], writes=[('cacc', ci)])
                if xbc_tail['f'] is not None:
                    xbc_tail['f']()

                def tail(c=c, ci=ci):
                    A('act', lambda e: e.activation(out=xcT[:, c, :], in_=cacc[:, ci, :], func=AF.Silu), reads=[('cacc', ci)], writes=[('R1', 'xc', c)])
                    A('act', lambda e: e.activation(out=prevB[:, :, c], in_=cfull[:, ci, 512:515], func=AF.Copy),
                      reads=[('cfull', ci)], writes=[('prevB', c)])
                xbc_tail['f'] = tail
            proj_fm(w_in[:, O_XBC + q * 512:O_XBC + (q + 1) * 512], evac_xbc)
            if q < 4:
                z_group(q)
        xbc_tail['f']()
        xbc_tail['f'] = None

        fill_state = {'in_ssd': True}

        def fill_bank():
            return (psf[3], ('psf', 3)) if fill_state['in_ssd'] else next_ps()

        front_slots = [(w_in[:, O_GA + q * 512:O_GA + (q + 1) * 512], 'ga', q) for q in range(2)] + \
                      [(w_a[:, q * 512:(q + 1) * 512], 'wa', q) for q in range(2)] + \
                      [(w_in[:, O_GB + q * 512:O_GB + (q + 1) * 512], 'gb', q) for q in range(2)]
        loaded = {}

        def front_load(i_):
            if i_ < len(front_slots) and i_ not in loaded:
                loaded[i_] = wload(front_slots[i_][0])

        def gen_front():
            for i_, (src, kind, q) in enumerate(front_slots):
                front_load(i_)
                front_load(i_ + 1)
                slot, wk = loaded[i_]
                rhs_t, rkey = (ainT, 'ainT') if kind == 'wa' else (uT, 'uT')
                for jj in range(4):
                    m = q * 4 + jj
                    ps, pk = fill_bank()
                    for k in range(8):
                        A('pe', lambda e, k=k: e.matmul(ps[:, :], lhsT=slot[:, k, jj * 128:(jj + 1) * 128], rhs=rhs_t[:, k, :], start=(k == 0), stop=(k == 7)),
                          reads=[wk[jj], (rkey, k)], writes=[pk], inc=(k == 7))

                    def evac(ps=ps, pk=pk, m=m, kind=kind):
                        if kind == 'ga':
                            A('act', lambda e: e.activation(out=sgT[:, m, :], in_=ps[:, :], func=AF.Copy), reads=[pk], writes=[('sgT', m)])
                            if m == 7:
                                for h2 in range(4):
                                    A('act', lambda e, h2=h2: e.activation(out=sgT[:, 2 * h2:2 * h2 + 2, :], in_=sgT[:, 2 * h2:2 * h2 + 2, :], func=AF.Sigmoid),
                                      reads=[('sgT', 2 * h2), ('sgT', 2 * h2 + 1)], writes=[('sgT', 2 * h2), ('sgT', 2 * h2 + 1)])
                        elif kind == 'wa':
                            A('dve', lambda e: e.tensor_tensor(out=sgT[:, m, :], in0=ps[:, :], in1=sgT[:, m, :], op=ALU.mult), reads=[pk, ('sgT', m)], writes=[('sgT', m)])
                        else:
                            A('act', lambda e: e.activation(out=sgbT[:, m, :], in_=ps[:, :], func=AF.Copy), reads=[pk], writes=[('sgbT', m)])
                    yield evac

        filler = gen_front()
        front_load(0)
        fstate = {'evac': None}

        def run_filler():
            if fstate['evac'] is not None:
                fstate['evac']()
                fstate['evac'] = None
            fstate['evac'] = next(filler, None)


        tr.retire('X')
        bank = lambda i_: (psf[i_], ('psf', i_))
        v3 = lambda ap: ap.rearrange("p (h d) -> p h d", d=64)
        v4 = lambda ap: ap.rearrange("p (a b) -> p a b", b=128)
        for s in range(4):
            sl = slice(s * 128, (s + 1) * 128)
            pcb, kcb = bank(2)
            for g in range(4):
                A('pe', lambda e, g=g: e.matmul(pcb[:, g * 128:(g + 1) * 128], lhsT=xcT[:, 16 + g, sl], rhs=xcT[:, 20 + g, sl], start=True, stop=True),
                  reads=[('R1', 'xc', 16 + g), ('R1', 'xc', 20 + g)], writes=[kcb], inc=(g == 3))
            A('act', lambda e: e.activation(out=CBs[:], in_=pcb[:, :], func=AF.Copy), reads=[kcb], writes=[('CBs',)])

            def stage1(k):
                pbc, kbc = bank(4 + k % 2)
                A('pe', lambda e: e.matmul(pbc[:, :], lhsT=ident_b[:], rhs=NEGM[:], start=True, stop=False),
                  reads=[('ident_b',), ('NEGM',)], writes=[kbc], inc=False)
                for j in range(4):
                    h = 4 * k + j
                    cs_ = slice(j * 128, (j + 1) * 128)
                    A('pe', lambda e, h=h, cs_=cs_: e.matmul(pbc[:, cs_], lhsT=Esel[:, h * 128:(h + 1) * 128], rhs=acsT4[:, s, 0, :], start=False, stop=False),
                      reads=[('Esel',), ('acsT4', s, 0)], writes=[kbc], inc=False)
                    A('pe', lambda e, h=h, cs_=cs_, j=j: e.matmul(pbc[:, cs_], lhsT=Esel[:, h * 128:(h + 1) * 128], rhs=acsT4[:, s, 1, :], start=False, stop=(j == 3)),
                      reads=[('Esel',), ('acsT4', s, 1)], writes=[kbc], inc=(j == 3))
                return pbc, kbc

            def epilogue_steps(g):
                gs = slice(g * 512, (g + 1) * 512)
                gi = g % 2
                pyd, kyd = bank(g % 2)
                pyo, kyo = bank(2)
                pst, kst = bank(2)

                def stepA():
                    A('pe', lambda e: e.matmul(pyo[:, :], lhsT=xcT[:, 20 + g, sl], rhs=hTb[:, gs], start=True, stop=True),
                      reads=[('R1', 'xc', 20 + g), ('hTb', g)], writes=[kyo])
                    A('dve', lambda e: e.tensor_tensor(out=v3(ytmp[:, gi, :]), in0=v3(xtm[:, 0, gs]),
                                                       in1=ddt4[:, s, g * 8:(g + 1) * 8].unsqueeze(2).to_broadcast([128, 8, 64]), op=ALU.mult),
                      reads=[('xtm', 0, g), ('ddt4',)], writes=[('ytmp', gi)])
                    A('dve', lambda e: e.tensor_tensor(out=v3(ysb[:, gi, :]), in0=v3(pyo[:, :]),
                                                       in1=eacs4[:, s, g * 8:(g + 1) * 8].unsqueeze(2).to_broadcast([128, 8, 64]), op=ALU.mult),
                      reads=[kyo, ('eacs4', s)], writes=[('ysb', gi)])

                def stepB():
                    A('dve', lambda e: e.tensor_tensor(out=ysb[:, gi, :], in0=ysb[:, gi, :], in1=pyd[:, :], op=ALU.add),
                      reads=[kyd, ('ysb', gi)], writes=[('ysb', gi)])
                    A('dve', lambda e: e.tensor_tensor(out=ysb[:, gi, :], in0=ysb[:, gi, :], in1=ytmp[:, gi, :], op=ALU.add),
                      reads=[('ytmp', gi), ('ysb', gi)], writes=[('ysb', gi)])
                    A('pe', lambda e: e.matmul(pst[:, :], lhsT=Btm[:, g * 128:(g + 1) * 128], rhs=xpp[:, gs], start=True, stop=True),
                      reads=[('R1', 'btm'), ('R1', 'xpp')], writes=[kst])

                def stepC():
                    A('dve', lambda e: e.tensor_tensor(out=ysb[:, gi, :], in0=ysb[:, gi, :], in1=zs[:, s, gs], op=ALU.mult),
                      reads=[('zs', s, g), ('ysb', gi)], writes=[('ysb', gi)])
                    A('dve', lambda e: e.memset(gss[:, g:g + 1], 0.0), writes=[('gss', g)])
                    A('dve', lambda e: e.tensor_tensor(out=v3(hT[:, gs]), in0=v3(hT[:, gs]),
                                                       in1=ealb4[:, s, g * 8:(g + 1) * 8].unsqueeze(2).to_broadcast([128, 8, 64]), op=ALU.mult),
                      reads=[('hT', g), ('ealb4', s)], writes=[('hT', g)])
                    A('dve', lambda e: e.tensor_tensor(out=hT[:, gs], in0=hT[:, gs], in1=pst[:, :], op=ALU.add),
                      reads=[('hT', g), kst], writes=[('hT', g)])

                def stepD():
                    A('act', lambda e: e.activation(out=ytmp[:, gi, :], in_=ysb[:, gi, :], func=AF.Square, accum_out=gss[:, g:g + 1]),
                      reads=[('ysb', gi), ('gss', g)], writes=[('ytmp', gi), ('gss', g)])
                    A('act', lambda e: e.activation(out=gss[:, g:g + 1], in_=gss[:, g:g + 1], func=AF.Ln, scale=1.0 / 512, bias=epsb[:, 0:1]),
                      reads=[('gss', g), ('epsb',)], writes=[('gss', g)])
                    A('act', lambda e: e.activation(out=grs[:, g:g + 1], in_=gss[:, g:g + 1], func=AF.Exp, scale=-0.5), reads=[('gss', g)], writes=[('grs', g)])
                    A('act', lambda e: e.activation(out=ynb[:, gs], in_=ysb[:, gi, :], func=AF.Copy, scale=grs[:, g:g + 1]),
                      reads=[('ysb', gi), ('grs', g)], writes=[('ynb', g)])
                    A('act', lambda e: e.activation(out=hTb[:, gs], in_=hT[:, gs], func=AF.Copy), reads=[('hT', g)], writes=[('hTb', g)])

                return [stepA, stepB, stepC, stepD]

            pending = []
            lastD = {'f': None}
            pb = {}
            for k in range(2):
                pb[k] = stage1(k)
            for q in range(4):
                pt, ptk = next_pt()
                for jj in range(4):
                    c = q * 4 + jj
                    A('pe', lambda e, c=c, jj=jj: e.transpose(pt[:, jj * 128:(jj + 1) * 128], xcT[:, c, sl], ident_b[:]),
                      reads=[('R1', 'xc', c), ('ident_b',)], writes=[ptk], inc=(jj == 3))
                A('dve', lambda e, q=q: e.tensor_tensor(out=v3(xtm[:, 0, q * 512:(q + 1) * 512]), in0=v3(pt),
                                                        in1=dtt[:, s, q * 8:(q + 1) * 8].unsqueeze(2).to_broadcast([128, 8, 64]), op=ALU.mult),
                  reads=[ptk, ('dtt',)], writes=[('xtm', 0, q)])
            pt, ptk = next_pt()
            for g in range(4):
                A('pe', lambda e, g=g: e.transpose(pt[:, g * 128:(g + 1) * 128], xcT[:, 16 + g, sl], ident_b[:]),
                  reads=[('R1', 'xc', 16 + g), ('ident_b',)], writes=[ptk], inc=(g == 3))
            A('act', lambda e: e.activation(out=Btm, in_=pt, func=AF.Copy), reads=[ptk], writes=[('R1', 'btm')])
            for k in range(8):
                g = k // 2
                ei = k % 2
                pbc, kbc = pb[k]
                for j in range(4):
                    h = 4 * k + j
                    A('act', lambda e, h=h, j=j: e.activation(out=e4[:, ei, j * 128:(j + 1) * 128], in_=pbc[:, j * 128:(j + 1) * 128], func=AF.Exp,
                                                              bias=nacs4[:, s, h:h + 1]),
                      reads=[kbc, ('nacs4', s)], writes=[('e4', ei, j)])
                A('dve', lambda e: e.tensor_tensor(out=v4(WT4[:, ei, :]), in0=v4(e4[:, ei, :]),
                                                   in1=CBs[:, g * 128:(g + 1) * 128].unsqueeze(1).to_broadcast([128, 4, 128]), op=ALU.mult),
                  reads=[('e4', ei, j_) for j_ in range(4)] + [('CBs',)], writes=[('WT4', ei)])
                pyd, kyd = bank(g % 2)
                for j in range(4):
                    h = 4 * k + j
                    r = h % 8
                    A('pe', lambda e, h=h, r=r, j=j: e.matmul(pyd[:, r * 64:(r + 1) * 64], lhsT=WT4[:, ei, j * 128:(j + 1) * 128], rhs=xtm[:, 0, h * 64:(h + 1) * 64],
                                                               start=True, stop=True),
                      reads=[('WT4', ei), ('xtm', 0, g)], writes=[kyd], inc=(j == 3))
                if k + 2 < 8:
                    pb[k + 2] = stage1(k + 2)
                if k == 0:
                    A('dve', lambda e: e.tensor_tensor(out=v3(xpp), in0=v3(xtm[:, 0, :]), in1=d24[:, s, :].unsqueeze(2).to_broadcast([128, 32, 64]), op=ALU.mult),
                      reads=[('xtm', 0, q) for q in range(4)] + [('d24', s)], writes=[('R1', 'xpp')])
                if k % 2 == 1:
                    sA, sB, sC, sD = epilogue_steps(g)
                    pending.extend([sA, sB, sC] + ([lastD['f']] if lastD['f'] is not None else []))
                    lastD['f'] = sD
                for _ in range(2):
                    if pending and k >= 2:
                        pending.pop(0)()
                run_filler()
            while pending:
                pending.pop(0)()
            lastD['f']()
            lastD['f'] = None
            for q in range(4):
                pt, ptk = next_pt()
                for jj in range(4):
                    c = q * 4 + jj
                    A('pe', lambda e, c=c, jj=jj: e.transpose(pt[:, jj * 128:(jj + 1) * 128], ynb[:, c * 128:(c + 1) * 128], ident_b[:]),
                      reads=[('ynb', q), ('ident_b',)], writes=[ptk], inc=(jj == 3))
                A('dve', lambda e, q=q: e.tensor_tensor(out=ynT[:, q * 4:(q + 1) * 4, sl], in0=pt.rearrange("p (a b) -> p a b", b=128),
                                                        in1=pf[:, P_SNG + q * 4:P_SNG + (q + 1) * 4].unsqueeze(2).to_broadcast([128, 4, 128]), op=ALU.mult),
                  reads=[ptk, ('pf',)], writes=[('ynT', q, s)])

        tail_jobs = []
        if last:
            def job_a():
                ps, pk = next_ps()
                A('pe', lambda e: e.transpose(ps[0:16, 0:128], prevA[:].rearrange("p r j -> p (r j)"), ident_f[:]),
                  reads=[('prevA', j) for j in range(8)] + [('ident_f',)], writes=[pk])
                A('act', lambda e: e.activation(out=sto[0:16, 0, :], in_=ps[0:16, 0:128], func=AF.Copy), reads=[pk], writes=[('sto', 0)])
                tr.dma('sp', na_p.rearrange("r (j p) -> (r j) p", p=128), sto[0:16, 0, :], 'd_sto0', reads=[('sto', 0)])

            def job_b():
                ps, pk = next_ps()
                A('pe', lambda e: e.transpose(ps[0:72, 0:128], prevB[:].rearrange("p r c -> p (r c)"), ident_f[:]),
                  reads=[('prevB', c) for c in range(24)] + [('ident_f',)], writes=[pk])
                A('act', lambda e: e.activation(out=sto[0:72, 1, :], in_=ps[0:72, 0:128], func=AF.Copy), reads=[pk], writes=[('sto', 1)])
                tr.dma('sp', nbc_p.rearrange("r (c p) -> (r c) p", p=128), sto[0:72, 1, :], 'd_sto1', reads=[('sto', 1)])

            def job_h(c):
                def f():
                    ps, pk = next_ps()
                    si = c % 2
                    A('pe', lambda e: e.transpose(ps[:, 0:128], hT[:, c * 128:(c + 1) * 128], ident_f[:]), reads=[('hT', c // 4), ('ident_f',)], writes=[pk])
                    A('act', lambda e: e.activation(out=sto[:, si, :], in_=ps[:, 0:128], func=AF.Copy), reads=[pk], writes=[('sto', si)])
                    tr.dma('sp', nssm_p[c * 128:(c + 1) * 128, :], sto[:, si, :], f'd_sto{si}', reads=[('sto', si)])
                return f
            tail_jobs = [job_a, job_b] + [job_h(c) for c in range(16)]

        def pop_tail():
            if tail_jobs:
                tail_jobs.pop(0)()

        if fstate['evac'] is not None:
            fstate['evac']()
            fstate['evac'] = None
        fill_state['in_ssd'] = False
        for ev in filler:
            ev()
        for h2 in range(4):
            A('act', lambda e, h2=h2: e.activation(out=sgbT[:, 2 * h2:2 * h2 + 2, :], in_=sgbT[:, 2 * h2:2 * h2 + 2, :], func=AF.Sigmoid),
              reads=[('sgbT', 2 * h2), ('sgbT', 2 * h2 + 1)], writes=[('sgbT', 2 * h2), ('sgbT', 2 * h2 + 1)])
        if not last:
            tr.retire(Qn)
            t0n = (ti + 1) * T
            for s in range(4):
                tr.dma('sp', bufs[Q_][:, s, :], xp[t0n + s * 128:t0n + (s + 1) * 128, :], f'd_xt{s}', writes=[(Qn, 'x', s)])
        for q in range(2):
            s0 = wload(w_b[0:1024, q * 512:(q + 1) * 512])
            s1 = wload(w_b[1024:2048, q * 512:(q + 1) * 512])
            for jj in range(4):
                m = q * 4 + jj
                ps, pk = next_ps()
                for kk in range(16):
                    slot, wk = (s0, s1)[kk // 8]
                    A('pe', lambda e, kk=kk, slot=slot, jj=jj: e.matmul(ps[:, :], lhsT=slot[:, kk % 8, jj * 128:(jj + 1) * 128], rhs=ynT[:, kk, :],
                                                                         start=(kk == 0), stop=(kk == 15)),
                      reads=[wk[jj]] + [('ynT', kk // 4, s) for s in range(4)], writes=[pk], inc=(kk == 15))
                hi = m % 2
                A('dve', lambda e, m=m: e.tensor_tensor(out=hvs[:, hi, :], in0=ps[:, :], in1=sgbT[:, m, :], op=ALU.mult), reads=[pk, ('sgbT', m)], writes=[('hvs', hi)])
                A('dve', lambda e, m=m: e.tensor_tensor(out=sgT[:, m, :], in0=sgT[:, m, :], in1=hvs[:, hi, :], op=ALU.add),
                  reads=[('hvs', hi), ('sgT', m)], writes=[('sgT', m)])
                pop_tail()
                pop_tail()
        while tail_jobs:
            pop_tail()
        for q in range(2):
            slot, wk = wload(w_o[:, q * 512:(q + 1) * 512])
            for s in range(4):
                ps, pk = next_ps()
                for k in range(8):
                    A('pe', lambda e, k=k, s=s: e.matmul(ps[:, :], lhsT=sgT[:, k, s * 128:(s + 1) * 128], rhs=slot[:, k, :], start=(k == 0), stop=(k == 7)),
                      reads=wk + [('sgT', k)], writes=[pk], inc=(k == 7))
                hi = (q * 4 + s) % 2
                A('dve', lambda e, q=q: e.tensor_tensor(out=hvs[:, hi, :], in0=ps[:, :], in1=g1bc[:, q * 512:(q + 1) * 512], op=ALU.mult),
                  reads=[pk, ('g1bc',)], writes=[('hvs', hi)])
                A('dve', lambda e, q=q, s=s: e.tensor_tensor(out=xt[:, s, q * 512:(q + 1) * 512], in0=xt[:, s, q * 512:(q + 1) * 512], in1=hvs[:, hi, :], op=ALU.add),
                  reads=[('hvs', hi), ('xt', s)], writes=[('xt', s)])

        tr.retire('R1')
        tr.retire('X')
        norm_to_T(A2, modT[:, 24:32], ('A2',), ('modT',))
        for q in range(8):
            def evac_h(jj, ps, pk, q=q):
                c = q * 4 + jj
                ri = c % 2
                A('act', lambda e: e.activation(out=relu[:, ri, :], in_=ps[:, :], func=AF.Relu), reads=[pk], writes=[('relu', ri)])
                A('dve', lambda e: e.tensor_tensor(out=hmT[:, c, :], in0=relu[:, ri, :], in1=relu[:, ri, :], op=ALU.mult),
                  reads=[('relu', ri)], writes=[('R1', 'hm', c)])
            proj_fm(w1[:, q * 512:(q + 1) * 512], evac_h)
            if q == 3 and not last:
                A('dve', lambda e: e.memset(ss_n[:, 0:4], 0.0), writes=[('ss_n',)])
                for s in range(4):
                    A('act', lambda e, s=s: e.activation(out=xnb[:, s, :], in_=bufs[Q_][:, s, :], func=AF.Square, accum_out=ss_n[:, s:s + 1]),
                      reads=[(Qn, 'x', s), ('ss_n',)], writes=[('xnb', s), ('ss_n',)])
                A('act', lambda e: e.activation(out=rstd_n[:, 4:8], in_=ss_n[:, 0:4], func=AF.Ln, scale=1.0 / D, bias=epsb[:, 0:1]),
                  reads=[('ss_n',), ('epsb',)], writes=[('rstd_nt',)])
                A('act', lambda e: e.activation(out=rstd_n[:, 0:4], in_=rstd_n[:, 4:8], func=AF.Exp, scale=-0.5), reads=[('rstd_nt',)], writes=[('rstd_n',)])
        for q in range(2):
            pss = [next_ps() for _ in range(4)]
            for kg in range(4):
                slot, wk = wload(w2[kg * 1024:(kg + 1) * 1024, q * 512:(q + 1) * 512])
                for s in range(4):
                    ps, pk = pss[s]
                    for k in range(8):
                        c = kg * 8 + k
                        A('pe', lambda e, k=k, s=s, c=c, ps=ps: e.matmul(ps[:, :], lhsT=hmT[:, c, s * 128:(s + 1) * 128], rhs=slot[:, k, :],
                                                                          start=(c == 0), stop=(c == 31)),
                          reads=wk + [('R1', 'hm', c)], writes=[pk], inc=(k == 7))
            for s in range(4):
                ps, pk = pss[s]
                hi = (q * 4 + s) % 2
                A('dve', lambda e, q=q, ps=ps: e.tensor_tensor(out=hvs[:, hi, :], in0=ps[:, :], in1=g2bc[:, q * 512:(q + 1) * 512], op=ALU.mult),
                  reads=[pk, ('g2bc',)], writes=[('hvs', hi)])
                A('dve', lambda e, q=q, s=s: e.tensor_tensor(out=xt[:, s, q * 512:(q + 1) * 512], in0=xt[:, s, q * 512:(q + 1) * 512], in1=hvs[:, hi, :], op=ALU.add),
                  reads=[('hvs', hi), ('xt', s)], writes=[('xt', s)])

        if not last:
            norm_to_T(A1, modT[:, 0:8], ('A1',), ('modT',), xb=bufs[Q_], xk=lambda s: (Qn, 'x', s), pre=True)

        A('dve', lambda e: e.memset(ss[:, 0:4], 0.0), writes=[('ss',)])
        for s in range(4):
            A('act', lambda e, s=s: e.activation(out=xnb[:, s, :], in_=xt[:, s, :], func=AF.Square, accum_out=ss[:, s:s + 1]),
              reads=[('xt', s), ('ss',)], writes=[('xnb', s), ('ss',)])
        A('act', lambda e: e.activation(out=rstd[:, 4:8], in_=ss[:, 0:4], func=AF.Ln, scale=1.0 / D, bias=epsb[:, 0:1]),
          reads=[('ss',), ('epsb',)], writes=[('rstd_t',)])
        A('act', lambda e: e.activation(out=rstd[:, 0:4], in_=rstd[:, 4:8], func=AF.Exp, scale=-0.5), reads=[('rstd_t',)], writes=[('rstd',)])
        for s in range(4):
            A('dve', lambda e, s=s: e.scalar_tensor_tensor(out=xt[:, s, :], in0=xt[:, s, :], scalar=rstd[:, s:s + 1], in1=pr[:, R_NFG:R_NFG + D],
                                                           op0=ALU.mult, op1=ALU.mult),
              reads=[('xt', s), ('rstd',), ('pr',)], writes=[('xt', s)])
            tr.dma('sp', yp[t0 + s * 128:t0 + (s + 1) * 128, :], xt[:, s, :], f'd_yp{s}', reads=[('xt', s)])

    wdump['on'] = False
    wdump['use'] = True
    tr.barrier()
    es2.close()
    es3 = ExitStack()

    def sb3(n, s_, dt=F32):
        return es3.enter_context(nc.sbuf_tensor(n, list(s_), dt))

    NSL2 = 7
    wsl['slots'] = [sb3(f"wss{i}", [128, 8, 512], BF16) for i in range(NSL2)]
    wsl['n'] = NSL2
    wsl['base'] = 3
    xsT_d = din("xsT", [128, 8 * NB])
    stAT_d = din("stAT", [128, 8 * 2 * NB])
    stBT_d = din("stBT", [128, 24 * 3 * NB])
    sssm_d = din("sssm", [NB, 2048, 128])
    ysT_d = dout("ysT", [128, 8 * NB])
    nasT_d = dout("nasT", [128, 8 * 2 * NB])
    nbsT_d = dout("nbsT", [128, 24 * 3 * NB])
    nssm_s_d = dout("nssm_s", [NB, 2048, 128])

    xs = sb3("xs", [128, 8, NB])
    stA = sb3("stA", [128, 8, 2, NB])
    stB = sb3("stB", [128, 24, 3, NB])
    nas = sb3("nas", [128, 8, 2, NB])
    nbs = sb3("nbs", [128, 24, 3, NB])
    A1s = sb3("A1s", [128, 8, NB])
    A2s = sb3("A2s", [128, 8, NB])
    sq = sb3("sq", [128, 16, NB])
    rs = sb3("rs", [128, NB])
    t8 = sb3("t8", [128, 8, NB])
    uTs = sb3("uTs", [128, 8, NB], BF16)
    projT = sb3("projT", [128, 81, NB])
    t24 = sb3("t24", [128, 24, NB])
    c24 = sb3("c24", [128, 24, NB])
    xcs = sb3("xcs", [128, 24, NB])
    ains = sb3("ains", [128, 8, NB], BF16)
    zss = sb3("zss", [128, 16, NB])
    dts = sb3("dts", [32, 33])
    dtm = sb3("dtm", [32, 2, NB])
    eacol = sb3("eacol", [32, 1])
    hl = sb3("hl", [32, 2, 33], BF16)
    hpT = sb3("hpT", [128, 16, 32])
    decP = sb3("decP", [128, 16, NB])
    dskP = sb3("dskP", [128, 16])
    xdt = sb3("xdt", [128, 16, NB])
    BCtm = sb3("BCtm", [16, 1024])
    BChl = sb3("BChl", [16, 2, 1024], BF16)
    hbuf = [sb3(f"hbuf{i}", [128, 16, 128]) for i in range(2)]
    t1bs = [sb3(f"t1b{i}", [128, 16, 128]) for i in range(2)]
    junk = sb3("junk", [128, 4, 128])
    yS = sb3("yS", [128, 16, NB])
    t16 = sb3("t16", [128, 16, NB])
    rg = sb3("rg", [128, 4, NB])
    ynTs = sb3("ynTs", [128, 16, NB], BF16)
    sga = sb3("sga", [128, 8, NB])
    sgb = sb3("sgb", [128, 8, NB])
    ma = sb3("ma", [128, 8, NB])
    ms = sb3("ms", [128, 8, NB], BF16)
    r4 = sb3("r4", [128, 4, NB])
    hms = sb3("hms", [128, 32, NB], BF16)

    tr.dma('sp', v2(xs), xsT_d[:, :], 'd_xs', writes=[('xs',)])
    tr.dma('sp', stA[:].rearrange("p a b c -> p (a b c)"), stAT_d[:, :], 'd_stA', writes=[('stA',)])
    tr.dma('sp', stB[:].rearrange("p a b c -> p (a b c)"), stBT_d[:, :], 'd_stB', writes=[('stB',)])
    MS = [('modS', cg) for cg in range(12)]
    A('dve', lambda e: e.scalar_tensor_tensor(out=A1s[:], in0=modS[:, 8:16, :], scalar=1.0, in1=bc3(pf[:, P_N1G:P_N1G + 8], 8), op0=ALU.add, op1=ALU.mult),
      reads=MS + [('pf',)], writes=[('A1s',)])
    A('dve', lambda e: e.scalar_tensor_tensor(out=A2s[:], in0=modS[:, 32:40, :], scalar=1.0, in1=bc3(pf[:, P_N2G:P_N2G + 8], 8), op0=ALU.add, op1=ALU.mult),
      reads=MS + [('pf',)], writes=[('A2s',)])

    def rms_s():
        A('dve', lambda e: e.tensor_tensor(out=sq[:, 0:8, :], in0=xs[:], in1=xs[:], op=ALU.mult), reads=[('xs',)], writes=[('sq',)])
        ps, pk = next_ps()
        for k in range(8):
            A('pe', lambda e, k=k: e.matmul(ps[:, 0:NB], lhsT=ones_f[:], rhs=sq[:, k, :], start=(k == 0), stop=(k == 7)),
              reads=[('ones_f',), ('sq',)], writes=[pk], inc=(k == 7))
        A('act', lambda e: e.activation(out=rs[:], in_=ps[:, 0:NB], func=AF.Ln, scale=1.0 / D, bias=epsb[:, 0:1]), reads=[pk, ('epsb',)], writes=[('rs',)])
        A('act', lambda e: e.activation(out=rs[:], in_=rs[:], func=AF.Exp, scale=-0.5), reads=[('rs',)], writes=[('rs',)])

    def mod_norm(As, Akey, sh0):
        rms_s()
        A('dve', lambda e: e.tensor_tensor(out=t8[:], in0=xs[:], in1=rs[:].unsqueeze(1).to_broadcast([128, 8, NB]), op=ALU.mult),
          reads=[('xs',), ('rs',)], writes=[('t8',)])
        A('dve', lambda e: e.tensor_tensor(out=t8[:], in0=t8[:], in1=As[:], op=ALU.mult), reads=[('t8',), Akey], writes=[('t8',)])
        A('dve', lambda e: e.tensor_tensor(out=uTs[:], in0=t8[:], in1=modS[:, sh0:sh0 + 8, :], op=ALU.add), reads=[('t8',)] + MS, writes=[('uTs',)])

    mod_norm(A1s, ('A1s',), 0)

    def proj_s(cols, idx0):
        slot, wk = wload(cols)
        ps, pk = next_ps()
        mm4(slot, wk, lambda k: uTs[:, k, :], [('uTs',)], 8, ps, pk)
        A('act', lambda e: e.activation(out=projT[:, idx0:idx0 + 4, :], in_=psv(ps), func=AF.Copy), reads=[pk], writes=[('projT', idx0 // 4 if idx0 < 64 else idx0)])

    for off in list(range(O_XBC, 8192, 512)) + list(range(O_Z, O_XBC, 512)):
        proj_s(w_in[:, off:off + 512], off // 128)
    ps, pk = next_ps()
    for k in range(8):
        A('pe', lambda e, k=k: e.matmul(ps[0:32, 0:NB], lhsT=wdt[:, k, :], rhs=uTs[:, k, :], start=(k == 0), stop=(k == 7)),
          reads=[('wdt',), ('uTs',)], writes=[pk], inc=(k == 7))
    late_proj = [(w_in[:, off:off + 512], off // 128) for off in range(0, O_Z, 512)]
    for q in range(2):
        late_proj.append((w_in[:, O_GA + q * 512:O_GA + (q + 1) * 512], 65 + 4 * q))
        late_proj.append((w_in[:, O_GB + q * 512:O_GB + (q + 1) * 512], 73 + 4 * q))
    PJ = [('projT', i_) for i_ in range(16)] + [('projT', 65), ('projT', 69), ('projT', 73), ('projT', 77)]

    wB = lambda tap: bc3(pf[:, P_CBW + tap * 24:P_CBW + tap * 24 + 24], 24)
    xbc = projT[:, 40:64, :]
    A('dve', lambda e: e.tensor_tensor(out=c24[:], in0=stB[:, :, 0, :], in1=wB(0), op=ALU.mult), reads=[('stB',), ('pf',), ('ains',)], writes=[('c24',)])
    for tap, src in ((1, stB[:, :, 1, :]), (2, stB[:, :, 2, :]), (3, xbc)):
        A('dve', lambda e, tap=tap, src=src: e.tensor_tensor(out=t24[:], in0=src, in1=wB(tap), op=ALU.mult), reads=[('stB',), ('pf',), ('nas', 1)] + PJ, writes=[('t24',)])
        A('dve', lambda e: e.tensor_tensor(out=c24[:], in0=c24[:], in1=t24[:], op=ALU.add), reads=[('c24',), ('t24',)], writes=[('c24',)])
    A('dve', lambda e: e.tensor_tensor(out=c24[:], in0=c24[:], in1=bc3(pf[:, P_CBB:P_CBB + 24], 24), op=ALU.add), reads=[('c24',), ('pf',)], writes=[('c24',)])
    A('act', lambda e: e.activation(out=xcs[:], in_=c24[:], func=AF.Silu), reads=[('c24',)], writes=[('xcs',)])
    A('act', lambda e: e.activation(out=nbs[:, :, 0, :], in_=stB[:, :, 1, :], func=AF.Copy), reads=[('stB',)], writes=[('nbs', 0)])
    A('act', lambda e: e.activation(out=nbs[:, :, 1, :], in_=stB[:, :, 2, :], func=AF.Copy), reads=[('stB',)], writes=[('nbs', 1)])
    A('act', lambda e: e.activation(out=nbs[:, :, 2, :], in_=xbc, func=AF.Copy), reads=PJ, writes=[('nbs', 2)])
    tr.dma('sp', nbsT_d[:, :], nbs[:].rearrange("p a b c -> p (a b c)"), 'd_nbs', reads=[('nbs', 0), ('nbs', 1), ('nbs', 2)])
    A('act', lambda e: e.activation(out=zss[:], in_=projT[:, 24:40, :], func=AF.Silu), reads=PJ, writes=[('zss',)])

    A('dve', lambda e: e.tensor_scalar(out=dtm[:, 0, :], in0=ps[0:32, 0:NB], scalar1=pf[0:32, P_HP:P_HP + 1], scalar2=None, op0=ALU.add),
      reads=[pk, ('pf',)], writes=[('dtm',)])
    A('dve', lambda e: e.scalar_tensor_tensor(out=dtm[:, 1, :], in0=dtm[:, 0, :], scalar=-1.0, in1=dtm[:, 0, :], op0=ALU.mult, op1=ALU.max),
      reads=[('dtm',)], writes=[('dtm1',)])
    A('act', lambda e: e.activation(out=dtm[:, 1, :], in_=dtm[:, 1, :], func=AF.Exp, scale=-1.0), reads=[('dtm1',)], writes=[('dtm1',)])
    A('act', lambda e: e.activation(out=dtm[:, 1, :], in_=dtm[:, 1, :], func=AF.Ln, bias=1.0), reads=[('dtm1',)], writes=[('dtm1',)])
    A('dve', lambda e: e.scalar_tensor_tensor(out=dts[:, 0:NB], in0=dtm[:, 0, :], scalar=0.0, in1=dtm[:, 1, :], op0=ALU.max, op1=ALU.add),
      reads=[('dtm',), ('dtm1',)], writes=[('dts',)])
    A('act', lambda e: e.activation(out=eacol[:], in_=pf[0:32, P_HP + 1:P_HP + 2], func=AF.Exp), reads=[('pf',)], writes=[('eacol',)])
    A('dve', lambda e: e.tensor_scalar(out=dts[:, NB:2 * NB], in0=dts[:, 0:NB], scalar1=eacol[:, 0:1], scalar2=-1.0, op0=ALU.mult, op1=ALU.mult),
      reads=[('dts',), ('eacol',)], writes=[('dts',)])
    A('dve', lambda e: e.tensor_copy(out=dts[:, 32:33], in_=pf[0:32, P_HP + 2:P_HP + 3]), reads=[('pf',), ('dts',)], writes=[('dts',)])
    A('dve', lambda e: e.tensor_copy(out=hl[:, 0, :], in_=dts[:]), reads=[('dts',)], writes=[('hl', 0)])
    A('dve', lambda e: e.tensor_tensor(out=dts[:], in0=dts[:], in1=hl[:, 0, :], op=ALU.subtract), reads=[('dts',), ('hl', 0)], writes=[('dts',)])
    A('dve', lambda e: e.tensor_copy(out=hl[:, 1, :], in_=dts[:]), reads=[('dts',)], writes=[('hl', 1)])
    ps1, pk1 = next_ps()
    ps2, pk2 = next_ps()
    for c in range(16):
        lh = Esel[:, 2 * c * 128 + 64:2 * c * 128 + 192]
        for i_ in range(2):
            A('pe', lambda e, c=c, i_=i_, lh=lh: e.matmul(ps1[:, c * 32:(c + 1) * 32], lhsT=lh, rhs=hl[:, i_, 0:32], start=(i_ == 0), stop=(i_ == 1)),
              reads=[('Esel',), ('hl', i_)], writes=[pk1], inc=False)
        for i_ in range(2):
            A('pe', lambda e, c=c, i_=i_, lh=lh: e.matmul(ps2[:, c:c + 1], lhsT=lh, rhs=hl[:, i_, 32:33], start=(i_ == 0), stop=(i_ == 1)),
              reads=[('Esel',), ('hl', i_)], writes=[pk2], inc=(i_ == 1 and c == 15))
    A('act', lambda e: e.activation(out=hpT[:].rearrange("p a b -> p (a b)"), in_=ps1[:, :], func=AF.Copy), reads=[pk1, pk2], writes=[('hpT',)])
    A('act', lambda e: e.activation(out=dskP[:], in_=ps2[:, 0:16], func=AF.Copy), reads=[pk2], writes=[('dskP',)])
    A('act', lambda e: e.activation(out=decP[:], in_=hpT[:, :, NB:2 * NB], func=AF.Exp), reads=[('hpT',)], writes=[('decP',)])
    A('dve', lambda e: e.tensor_tensor(out=xdt[:], in0=xcs[:, 0:16, :], in1=hpT[:, :, 0:NB], op=ALU.mult), reads=[('xcs',), ('hpT',)], writes=[('xdt',)])
    for half in range(2):
        ps, pk = next_ps()
        for i_ in range(4):
            A('pe', lambda e, i_=i_: e.transpose(ps[0:NB, i_ * 128:(i_ + 1) * 128], xcs[:, 16 + half * 4 + i_, :], ident_f[:]),
              reads=[('xcs',), ('ident_f',)], writes=[pk], inc=(i_ == 3))
        A('act', lambda e: e.activation(out=BCtm[:, half * 512:(half + 1) * 512], in_=ps[0:NB, :], func=AF.Copy), reads=[pk], writes=[('BCtm', half)])
    A('dve', lambda e: e.tensor_copy(out=BChl[:, 0, :], in_=BCtm[:]), reads=[('BCtm', 0), ('BCtm', 1)], writes=[('BChl', 0)])
    A('dve', lambda e: e.tensor_tensor(out=BCtm[:], in0=BCtm[:], in1=BChl[:, 0, :], op=ALU.subtract), reads=[('BCtm', 0), ('BCtm', 1), ('BChl', 0)], writes=[('BCtm', 0), ('BCtm', 1)])
    A('dve', lambda e: e.tensor_copy(out=BChl[:, 1, :], in_=BCtm[:]), reads=[('BCtm', 0), ('BCtm', 1)], writes=[('BChl', 1)])
    A('dve', lambda e: e.memset(yS[:], 0.0), writes=[('yS', c_) for c_ in range(16)])

    for b in range(NB):
        bi = b % 2
        hb = hbuf[bi]
        if late_proj:
            proj_s(*late_proj.pop(0))
        if b == 0:
            tr.dma('sp', hb[:], sssm_d[0].rearrange("(c q) n -> q c n", q=128), f'd_h{bi}', writes=[('hbuf', bi, c_) for c_ in range(16)])
        if b + 1 < NB:
            tr.dma('sp', hbuf[1 - bi][:], sssm_d[b + 1].rearrange("(c q) n -> q c n", q=128), f'd_h{1 - bi}',
                   writes=[('hbuf', 1 - bi, c_) for c_ in range(16)])
        pB, kB = psf[bi], ('psf', bi)
        pC, kC = psf[2 + bi], ('psf', 2 + bi)
        lh = Esel[0:NB, b * 128:(b + 1) * 128]
        for i_ in range(2):
            A('pe', lambda e, i_=i_: e.matmul(pB[:, :], lhsT=lh, rhs=BChl[:, i_, 0:512], start=(i_ == 0), stop=(i_ == 1)),
              reads=[('Esel',), ('BChl', i_)], writes=[kB], inc=(i_ == 1))
        for i_ in range(2):
            A('pe', lambda e, i_=i_: e.matmul(pC[:, :], lhsT=lh, rhs=BChl[:, i_, 512:1024], start=(i_ == 0), stop=(i_ == 1)),
              reads=[('Esel',), ('BChl', i_)], writes=[kC], inc=(i_ == 1))
        for c in range(16):
            g = c // 4
            A('act', lambda e, c=c, g=g: e.activation(out=t1bs[bi][:, c, :], in_=pB[:, g * 128:(g + 1) * 128], func=AF.Copy, scale=xdt[:, c, b:b + 1]),
              reads=[kB, ('xdt',)], writes=[('t1b', bi, c)])
        A('dve', lambda e: e.tensor_tensor(out=hb[:], in0=hb[:], in1=decP[:, :, b:b + 1].to_broadcast([128, 16, 128]), op=ALU.mult),
          reads=[('hbuf', bi, c_) for c_ in range(16)] + [('decP',)], writes=[('hbuf', bi, c_) for c_ in range(16)])
        A('dve', lambda e: e.tensor_tensor(out=hb[:].rearrange("p c n -> p (c n)"), in0=hb[:].rearrange("p c n -> p (c n)"),
                                           in1=t1bs[bi][:].rearrange("p c n -> p (c n)"), op=ALU.add),
          reads=[('hbuf', bi, c_) for c_ in range(16)] + [('t1b', bi, c_) for c_ in range(16)], writes=[('hbuf', bi, c_) for c_ in range(16)])
        for c in range(16):
            g = c // 4
            A('dve', lambda e, c=c, g=g: e.scalar_tensor_tensor(out=junk[:, c % 4, :], in0=hb[:, c, :], scalar=1.0, in1=pC[:, g * 128:(g + 1) * 128], op0=ALU.mult, op1=ALU.mult,
                                                                accum_out=yS[:, c, b:b + 1]),
              reads=[('hbuf', bi, c), kC, ('yS', c)], writes=[('yS', c), ('junk', c % 4)])
        tr.dma('sp', nssm_s_d[b].rearrange("(c q) n -> q c n", q=128), hb[:], f'd_ho{bi}', reads=[('hbuf', bi, c_) for c_ in range(16)])

    while late_proj:
        proj_s(*late_proj.pop(0))
    wA = lambda tap: bc3(pf[:, P_CAW + tap * 8:P_CAW + tap * 8 + 8], 8)
    ci = t24[:, 0:8, :]
    co = c24[:, 0:8, :]
    A('dve', lambda e: e.tensor_tensor(out=ci, in0=projT[:, 8:16, :], in1=projT[:, 16:24, :], op=ALU.mult), reads=PJ, writes=[('t24',)])
    A('dve', lambda e: e.tensor_tensor(out=co, in0=stA[:, :, 0, :], in1=wA(0), op=ALU.mult), reads=[('stA',), ('pf',)], writes=[('c24',)])
    A('dve', lambda e: e.tensor_tensor(out=t8[:], in0=stA[:, :, 1, :], in1=wA(1), op=ALU.mult), reads=[('stA',), ('pf',)], writes=[('t8',)])
    A('dve', lambda e: e.tensor_tensor(out=co, in0=co, in1=t8[:], op=ALU.add), reads=[('c24',), ('t8',)], writes=[('c24',)])
    A('dve', lambda e: e.tensor_tensor(out=t8[:], in0=ci, in1=wA(2), op=ALU.mult), reads=[('t24',), ('pf',)], writes=[('t8',)])
    A('dve', lambda e: e.tensor_tensor(out=co, in0=co, in1=t8[:], op=ALU.add), reads=[('c24',), ('t8',)], writes=[('c24',)])
    A('dve', lambda e: e.tensor_tensor(out=ains[:], in0=projT[:, 0:8, :], in1=co, op=ALU.mult), reads=PJ + [('c24',)], writes=[('ains',)])
    A('act', lambda e: e.activation(out=nas[:, :, 0, :], in_=stA[:, :, 1, :], func=AF.Copy), reads=[('stA',)], writes=[('nas', 0)])
    A('act', lambda e: e.activation(out=nas[:, :, 1, :], in_=ci, func=AF.Copy), reads=[('t24',)], writes=[('nas', 1)])
    tr.dma('sp', nasT_d[:, :], nas[:].rearrange("p a b c -> p (a b c)"), 'd_nas', reads=[('nas', 0), ('nas', 1)])

    A('dve', lambda e: e.tensor_tensor(out=t16[:], in0=xcs[:, 0:16, :], in1=bc3(dskP[:], 16), op=ALU.mult), reads=[('xcs',), ('dskP',)], writes=[('t16',)])
    A('dve', lambda e: e.tensor_tensor(out=yS[:], in0=yS[:], in1=t16[:], op=ALU.add), reads=[('yS', c_) for c_ in range(16)] + [('t16',)], writes=[('yS', c_) for c_ in range(16)])
    A('dve', lambda e: e.tensor_tensor(out=yS[:], in0=yS[:], in1=zss[:], op=ALU.mult), reads=[('yS', c_) for c_ in range(16)] + [('zss',)], writes=[('yS', c_) for c_ in range(16)])
    A('dve', lambda e: e.tensor_tensor(out=sq[:], in0=yS[:], in1=yS[:], op=ALU.mult), reads=[('yS', c_) for c_ in range(16)], writes=[('sq',)])
    ps, pk = next_ps()
    for g in range(4):
        for i_ in range(4):
            A('pe', lambda e, g=g, i_=i_: e.matmul(ps[:, g * NB:(g + 1) * NB], lhsT=ones_f[:], rhs=sq[:, g * 4 + i_, :], start=(i_ == 0), stop=(i_ == 3)),
              reads=[('ones_f',), ('sq',)], writes=[pk], inc=(g == 3 and i_ == 3))
    rgv = rg[:].rearrange("p a b -> p (a b)")
    A('act', lambda e: e.activation(out=rgv, in_=ps[:, 0:4 * NB], func=AF.Ln, scale=1.0 / 512, bias=epsb[:, 0:1]), reads=[pk, ('epsb',)], writes=[('rg',)])
    A('act', lambda e: e.activation(out=rgv, in_=rgv, func=AF.Exp, scale=-0.5), reads=[('rg',)], writes=[('rg',)])
    for g in range(4):
        A('dve', lambda e, g=g: e.tensor_tensor(out=yS[:, g * 4:(g + 1) * 4, :], in0=yS[:, g * 4:(g + 1) * 4, :],
                                                in1=rg[:, g, :].unsqueeze(1).to_broadcast([128, 4, NB]), op=ALU.mult),
          reads=[('yS', c_) for c_ in range(16)] + [('rg',)], writes=[('yS', c_) for c_ in range(16)])
    A('dve', lambda e: e.tensor_tensor(out=ynTs[:], in0=yS[:], in1=bc3(pf[:, P_SNG:P_SNG + 16], 16), op=ALU.mult), reads=[('yS', c_) for c_ in range(16)] + [('pf',)], writes=[('ynTs',)])

    A('act', lambda e: e.activation(out=sga[:], in_=projT[:, 65:73, :], func=AF.Sigmoid), reads=PJ, writes=[('sga',)])
    A('act', lambda e: e.activation(out=sgb[:], in_=projT[:, 73:81, :], func=AF.Sigmoid), reads=PJ, writes=[('sgb',)])
    for q in range(2):
        slot, wk = wload(w_a[:, q * 512:(q + 1) * 512])
        ps, pk = next_ps()
        mm4(slot, wk, lambda k: ains[:, k, :], [('ains',)], 8, ps, pk)
        A('dve', lambda e, q=q: e.tensor_tensor(out=ma[:, 4 * q:4 * q + 4, :], in0=psv(ps), in1=sga[:, 4 * q:4 * q + 4, :], op=ALU.mult),
          reads=[pk, ('sga',)], writes=[('ma', q)])
    for q in range(2):
        s0 = wload(w_b[0:1024, q * 512:(q + 1) * 512])
        s1 = wload(w_b[1024:2048, q * 512:(q + 1) * 512])
        ps, pk = next_ps()
        mm4(None, None, lambda kk: ynTs[:, kk, :], [('ynTs',)], 16, ps, pk, lhs_slots=[s0, s1])
        A('dve', lambda e, q=q: e.tensor_tensor(out=r4[:], in0=psv(ps), in1=sgb[:, 4 * q:4 * q + 4, :], op=ALU.mult), reads=[pk, ('sgb',)], writes=[('r4',)])
        A('dve', lambda e, q=q: e.tensor_tensor(out=ma[:, 4 * q:4 * q + 4, :], in0=ma[:, 4 * q:4 * q + 4, :], in1=r4[:], op=ALU.add),
          reads=[('ma', q), ('r4',)], writes=[('ma', q)])
    A('dve', lambda e: e.tensor_copy(out=ms[:], in_=ma[:]), reads=[('ma', 0), ('ma', 1)], writes=[('ms',)])
    for q in range(2):
        slot, wk = wload(w_o[:, q * 512:(q + 1) * 512])
        ps, pk = next_ps()
        mm4(slot, wk, lambda k: ms[:, k, :], [('ms',)], 8, ps, pk)
        A('dve', lambda e, q=q: e.tensor_tensor(out=r4[:], in0=psv(ps), in1=modS[:, 16 + 4 * q:16 + 4 * q + 4, :], op=ALU.mult), reads=[pk] + MS, writes=[('r4',)])
        A('dve', lambda e, q=q: e.tensor_tensor(out=xs[:, 4 * q:4 * q + 4, :], in0=xs[:, 4 * q:4 * q + 4, :], in1=r4[:], op=ALU.add),
          reads=[('xs',), ('r4',)], writes=[('xs',)])

    mod_norm(A2s, ('A2s',), 24)
    for q in range(8):
        slot, wk = wload(w1[:, q * 512:(q + 1) * 512])
        ps, pk = next_ps()
        mm4(slot, wk, lambda k: uTs[:, k, :], [('uTs',)], 8, ps, pk)
        A('act', lambda e: e.activation(out=r4[:], in_=psv(ps), func=AF.Relu), reads=[pk], writes=[('r4',)])
        A('dve', lambda e, q=q: e.tensor_tensor(out=hms[:, 4 * q:4 * q + 4, :], in0=r4[:], in1=r4[:], op=ALU.mult), reads=[('r4',)], writes=[('hms', q)])
    for q in range(2):
        psA, kA = next_ps()
        for kg in range(4):
            slot, wk = wload(w2[kg * 1024:(kg + 1) * 1024, q * 512:(q + 1) * 512])
            for k in range(8):
                c = kg * 8 + k
                A('pe', lambda e, k=k, c=c: e.matmul(psA[0:NB, :], lhsT=hms[:, c, :], rhs=slot[:, k, :], start=(c == 0), stop=(c == 31)),
                  reads=list(wk) + [('hms', c // 4)], writes=[kA], inc=(k == 7))
        ps, pk = next_ps()
        tm_to_fm(psA, kA, ps, pk)
        A('dve', lambda e, q=q: e.tensor_tensor(out=r4[:], in0=psv(ps), in1=modS[:, 40 + 4 * q:40 + 4 * q + 4, :], op=ALU.mult), reads=[pk] + MS, writes=[('r4',)])
        A('dve', lambda e, q=q: e.tensor_tensor(out=xs[:, 4 * q:4 * q + 4, :], in0=xs[:, 4 * q:4 * q + 4, :], in1=r4[:], op=ALU.add),
          reads=[('xs',), ('r4',)], writes=[('xs',)])

    rms_s()
    A('dve', lambda e: e.tensor_tensor(out=t8[:], in0=xs[:], in1=rs[:].unsqueeze(1).to_broadcast([128, 8, NB]), op=ALU.mult),
      reads=[('xs',), ('rs',)], writes=[('t8',)])
    A('dve', lambda e: e.tensor_tensor(out=t8[:], in0=t8[:], in1=bc3(pf[:, P_NFG:P_NFG + 8], 8), op=ALU.mult), reads=[('t8',), ('pf',)], writes=[('t8',)])
    tr.dma('sp', ysT_d[:, :], v2(t8), 'd_ys', reads=[('t8',)])

    tr.finish('sp')
    es3.close()
    es.close()
    return nc


_NC = None


def _get_nc():
    global _NC
    if _NC is None:
        _NC = build()
    return _NC


def kernel(x_prompt, x_sample, c_prompt, c_sample, state_shortconv, state_ssm_conv, state_ssm,
           w_ada, b_ada, norm1_g, w_in, conv_a_w, w_a_out, conv_b_w, conv_b_b, dt_bias, a_log,
           d_skip, ssm_norm_g, w_b_out, w_o, norm2_g, w_mlp1, w_mlp2, norm_f_g):
    f = lambda a: np.ascontiguousarray(np.asarray(a, dtype=np.float32))
    fm = lambda v: np.asarray(v, np.float32).reshape(-1, 128).T
    pfm = np.concatenate([
        fm(norm1_g[0]),
        np.concatenate([fm(conv_a_w[0][t]) for t in range(3)], axis=1),
        np.concatenate([fm(conv_b_w[0][t]) for t in range(4)], axis=1),
        fm(conv_b_b[0]), fm(ssm_norm_g[0]), fm(norm2_g[0]), fm(b_ada[0]), fm(norm_f_g),
        np.concatenate([np.stack([np.asarray(v[0], np.float32) for v in (dt_bias, a_log, d_skip)], axis=1), np.zeros((96, 3), np.float32)], axis=0),
    ], axis=1)
    assert pfm.shape == (128, NPF)
    b_ada0 = np.asarray(b_ada[0], np.float32)
    prow = np.concatenate([np.asarray(dt_bias[0], np.float32), np.asarray(a_log[0], np.float32), np.asarray(d_skip[0], np.float32),
                           np.asarray(norm_f_g, np.float32)])[None, :]
    bg12 = np.stack([b_ada0[2048:3072], b_ada0[5120:6144]])
    assert prow.shape == (1, NPR)
    shared = {"pfm": f(pfm), "prow": f(prow), "bg12": f(bg12), "w_ada": f(w_ada[0]), "w_in": f(w_in[0]), "w_a_out": f(w_a_out[0]),
              "w_b_out": f(w_b_out[0]), "w_o": f(w_o[0]), "w_mlp1": f(w_mlp1[0]), "w_mlp2": f(w_mlp2[0])}
    in_maps = []
    for c in range(NCORES):
        m = dict(shared)
        m["xp"] = f(x_prompt[c])
        m["cpT"] = f(fm(c_prompt[c]))
        rs_ = slice(c * NSB, (c + 1) * NSB)
        m["xsT"] = f(np.asarray(x_sample[rs_, 0, :], np.float32).reshape(NSB, 8, 128).transpose(2, 1, 0).reshape(128, -1))
        m["csT"] = f(np.asarray(c_sample[rs_], np.float32).reshape(NSB, 8, 128).transpose(2, 1, 0).reshape(128, -1))
        m["stAT"] = f(np.asarray(state_shortconv[0, rs_], np.float32).reshape(NSB, 2, 8, 128).transpose(3, 2, 1, 0).reshape(128, -1))
        m["stBT"] = f(np.asarray(state_ssm_conv[0, rs_], np.float32).reshape(NSB, 3, 24, 128).transpose(3, 2, 1, 0).reshape(128, -1))
        m["sssm"] = f(np.asarray(state_ssm[0, rs_], np.float32).reshape(NSB, 2048, 128))
        in_maps.append(m)
    nc = _get_nc()
    res = run_bass_kernel_spmd(nc, in_maps, core_ids=list(range(NCORES)))
    R = res.results
    y_prompt = np.stack([R[c]["yp"] for c in range(NCORES)]).astype(np.float32)
    na_p = np.stack([R[c]["na_p"] for c in range(NCORES)])[None].astype(np.float32)
    nbc_p = np.stack([R[c]["nbc_p"] for c in range(NCORES)])[None].astype(np.float32)
    nssm_p = np.stack([R[c]["nssm_p"].reshape(32, 64, 128) for c in range(NCORES)])[None].astype(np.float32)
    y_sample = np.concatenate([R[c]["ysT"].reshape(128, 8, NSB).transpose(2, 1, 0).reshape(NSB, 1, D) for c in range(NCORES)]).astype(np.float32)
    na_s = np.concatenate([R[c]["nasT"].reshape(128, 8, 2, NSB).transpose(3, 2, 1, 0).reshape(NSB, 2, D) for c in range(NCORES)])[None].astype(np.float32)
    nbc_s = np.concatenate([R[c]["nbsT"].reshape(128, 24, 3, NSB).transpose(3, 2, 1, 0).reshape(NSB, 3, 3072) for c in range(NCORES)])[None].astype(np.float32)
    nssm_s = np.concatenate([R[c]["nssm_s"].reshape(NSB, 32, 64, 128) for c in range(NCORES)])[None].astype(np.float32)
    return (y_prompt, y_sample, na_p, nbc_p, nssm_p, na_s, nbc_s, nssm_s)
```
